# Optimizing a Trainium2 kernel written in Bass

```python
import math
import jax, jax.numpy as jnp
from jax import lax
import numpy as np

D_MODEL = 1024
BATCH = 1
SEQ = 16384
DEPTH = 1

HEAD_DIM = 64
HEADS_PER_GROUP = 8
DILATED_GROUPS = ((128, 1), (512, 4), (2048, 16))
N_GROUPS = len(DILATED_GROUPS)
N_ATTN_HEADS = N_GROUPS * HEADS_PER_GROUP
ATTN_W = N_ATTN_HEADS * HEAD_DIM
ATTN_OUT_W = HEADS_PER_GROUP * HEAD_DIM
BLOCK = 128
REL_BUCKETS = 32
REL_MAX_DISTANCE = 2048
CONV_CHANNELS = D_MODEL
CONV_WIDTH = 31
FFN_HIDDEN = -(-8 * D_MODEL // (3 * 256)) * 256
IN_W = 3 * ATTN_W + 2 * CONV_CHANNELS + 2 * D_MODEL
RMS_EPS = 1e-6
LN_EPS = 1e-5
NEG_INF = -1e30

kernel_name = "hybrid_dilated_attn_conformer_conv_gated_block"


def rms_norm(x, g):
    xf = x.astype(jnp.float32)
    y = xf * lax.rsqrt(jnp.mean(xf * xf, axis=-1, keepdims=True) + RMS_EPS)
    return (y * g.astype(jnp.float32)).astype(x.dtype)


def layer_norm(x, g, b):
    xf = x.astype(jnp.float32)
    mu = jnp.mean(xf, axis=-1, keepdims=True)
    xc = xf - mu
    y = xc * lax.rsqrt(jnp.mean(xc * xc, axis=-1, keepdims=True) + LN_EPS)
    return (y * g.astype(jnp.float32) + b.astype(jnp.float32)).astype(x.dtype)


def rel_bucket(dist):
    max_exact = REL_BUCKETS // 2
    d = jnp.maximum(dist, 0)
    df = jnp.maximum(d, 1).astype(jnp.float32)
    large = max_exact + (jnp.log(df / max_exact) / math.log(REL_MAX_DISTANCE / max_exact)
                         * (REL_BUCKETS - max_exact)).astype(jnp.int32)
    large = jnp.minimum(large, REL_BUCKETS - 1)
    return jnp.where(d < max_exact, d, large)


def dilated_group_attention(q, k, v, bias_tab, window, dilation):
    B, S, H, Dh = q.shape
    span = window // dilation
    L = S // dilation
    nb = -(-L // BLOCK)
    Lp = nb * BLOCK
    n_prev = -(-span // BLOCK)
    kb_len = (n_prev + 1) * BLOCK

    def to_sub(t, front):
        t = t.reshape(B, L, dilation, H, Dh).transpose(0, 2, 1, 3, 4)
        return jnp.pad(t, ((0, 0), (0, 0), (front, Lp - L), (0, 0), (0, 0)))

    qb = to_sub(q, 0).reshape(B, dilation, nb, BLOCK, H, Dh)

    def band(t):
        tb = to_sub(t, n_prev * BLOCK).reshape(B, dilation, nb + n_prev, BLOCK, H, Dh)
        return jnp.concatenate([tb[:, :, j:j + nb] for j in range(n_prev + 1)], axis=3)

    kb, vb = band(k), band(v)
    a = jnp.arange(BLOCK, dtype=jnp.int32)[:, None]
    c = jnp.arange(kb_len, dtype=jnp.int32)[None, :]
    offset = a - c + n_prev * BLOCK
    bias = bias_tab[rel_bucket(offset * dilation)].astype(jnp.float32).transpose(2, 0, 1)
    kj = (jnp.arange(nb, dtype=jnp.int32)[:, None, None] - n_prev) * BLOCK + c[None]
    valid = (offset >= 0) & (offset <= span) & (kj >= 0)

    s = jnp.einsum('brnqhd,brnkhd->brnhqk', qb, kb).astype(jnp.float32) * (Dh ** -0.5) + bias
    s = jnp.where(valid[:, None], s, NEG_INF)
    lse = jax.nn.logsumexp(s, axis=-1)
    p = jnp.exp(s - lse[..., None])
    o = jnp.einsum('brnhqk,brnkhd->brnqhd', p, vb.astype(jnp.float32))
    o = o.reshape(B, dilation, Lp, H, Dh)[:, :, :L].transpose(0, 2, 1, 3, 4).reshape(B, S, H, Dh)
    lse = lse.transpose(0, 1, 2, 4, 3).reshape(B, dilation, Lp, H)[:, :, :L]
    lse = lse.transpose(0, 2, 1, 3).reshape(B, S, H)
    return o, lse


def dilated_attention_mixer(q, k, v, rel_bias_table, w_attn_out):
    B, S = q.shape[0], q.shape[1]
    outs, lses = [], []
    for g, (window, dilation) in enumerate(DILATED_GROUPS):
        tab = rel_bias_table[:, g * HEADS_PER_GROUP:(g + 1) * HEADS_PER_GROUP]
        o_g, lse_g = dilated_group_attention(q[:, :, g], k[:, :, g], v[:, :, g], tab, window, dilation)
        outs.append(o_g)
        lses.append(lse_g)
    alpha = jax.nn.softmax(jnp.stack(lses, axis=0), axis=0)
    o = jnp.einsum('gbsh,gbshd->bshd', alpha, jnp.stack(outs, axis=0))
    o = o.reshape(B, S, ATTN_OUT_W).astype(w_attn_out.dtype)
    return o @ w_attn_out


def conformer_conv_mixer(glu_in, b_glu, w_dw, b_dw, g_ln, b_ln, w_conv_out, b_conv_out):
    h = glu_in + b_glu
    u, gate = jnp.split(h, 2, axis=-1)
    u = u * jax.nn.sigmoid(gate)
    u = lax.conv_general_dilated(
        u, w_dw.reshape(CONV_WIDTH, 1, CONV_CHANNELS).astype(u.dtype),
        window_strides=(1,), padding=[(CONV_WIDTH - 1, 0)],
        dimension_numbers=('NWC', 'WIO', 'NWC'),
        feature_group_count=CONV_CHANNELS) + b_dw
    u = jax.nn.silu(layer_norm(u, g_ln, b_ln))
    return u @ w_conv_out + b_conv_out


def swiglu_ffn(h, w_ffn_in, w_ffn_out):
    gate, up = jnp.split(h @ w_ffn_in, 2, axis=-1)
    return (jax.nn.silu(gate) * up) @ w_ffn_out


def setup_inputs(seed: int = 0) -> dict:
    key = jax.random.key(seed)
    ks = jax.random.split(key, 20)
    f32 = jnp.float32

    def nrm(k, shape, scale):
        return jax.random.normal(k, shape, f32) * scale

    def gain(k, shape):
        return 1.0 + 0.05 * jax.random.normal(k, shape, f32)

    return {
        "x": jax.random.normal(ks[0], (BATCH, SEQ, D_MODEL), f32),
        "rel_bias_table": nrm(ks[1], (REL_BUCKETS, N_ATTN_HEADS), 0.2),
        "g_pre_mix": gain(ks[2], (DEPTH, D_MODEL)),
        "w_in": nrm(ks[3], (DEPTH, D_MODEL, IN_W), D_MODEL ** -0.5),
        "b_glu": nrm(ks[4], (DEPTH, 2 * CONV_CHANNELS), 0.02),
        "w_dw": nrm(ks[5], (DEPTH, CONV_WIDTH, CONV_CHANNELS), CONV_WIDTH ** -0.5),
        "b_dw": nrm(ks[6], (DEPTH, CONV_CHANNELS), 0.02),
        "g_conv_ln": gain(ks[7], (DEPTH, CONV_CHANNELS)),
        "b_conv_ln": nrm(ks[8], (DEPTH, CONV_CHANNELS), 0.02),
        "w_conv_out": nrm(ks[9], (DEPTH, CONV_CHANNELS, D_MODEL), CONV_CHANNELS ** -0.5),
        "b_conv_out": nrm(ks[10], (DEPTH, D_MODEL), 0.02),
        "w_attn_out": nrm(ks[11], (DEPTH, ATTN_OUT_W, D_MODEL), ATTN_OUT_W ** -0.5),
        "w_mix_out": nrm(ks[12], (DEPTH, D_MODEL, D_MODEL), D_MODEL ** -0.5),
        "g_post_mix": gain(ks[13], (DEPTH, D_MODEL)),
        "g_pre_ffn": gain(ks[14], (DEPTH, D_MODEL)),
        "w_ffn_in": nrm(ks[15], (DEPTH, D_MODEL, 2 * FFN_HIDDEN), D_MODEL ** -0.5),
        "w_ffn_out": nrm(ks[16], (DEPTH, FFN_HIDDEN, D_MODEL), FFN_HIDDEN ** -0.5),
        "g_post_ffn": gain(ks[17], (DEPTH, D_MODEL)),
    }


def reference(x, rel_bias_table, g_pre_mix, w_in, b_glu, w_dw, b_dw, g_conv_ln, b_conv_ln,
              w_conv_out, b_conv_out, w_attn_out, w_mix_out, g_post_mix, g_pre_ffn,
              w_ffn_in, w_ffn_out, g_post_ffn):
    B, S, D = x.shape
    for l in range(DEPTH):
        h = rms_norm(x, g_pre_mix[l])
        z = h @ w_in[l]
        q, k, v, glu_in, z_ga, z_gc = jnp.split(
            z, np.cumsum([ATTN_W, ATTN_W, ATTN_W, 2 * CONV_CHANNELS, D_MODEL]).tolist(), axis=-1)
        shp = (B, S, N_GROUPS, HEADS_PER_GROUP, HEAD_DIM)
        y_attn = dilated_attention_mixer(q.reshape(shp), k.reshape(shp), v.reshape(shp),
                                         rel_bias_table, w_attn_out[l])
        y_conv = conformer_conv_mixer(glu_in, b_glu[l], w_dw[l], b_dw[l], g_conv_ln[l],
                                      b_conv_ln[l], w_conv_out[l], b_conv_out[l])
        merged = jax.nn.sigmoid(z_ga) * y_attn + jax.nn.sigmoid(z_gc) * y_conv
        x = x + rms_norm(merged @ w_mix_out[l], g_post_mix[l])
        h = rms_norm(x, g_pre_ffn[l])
        x = x + rms_norm(swiglu_ffn(h, w_ffn_in[l], w_ffn_out[l]), g_post_ffn[l])
    return x
```

```python
import math
import os
from contextlib import ExitStack

import numpy as np
import concourse.bass as bass
import concourse.mybir as mybir
from concourse.bass_utils import run_bass_kernel_spmd

F32 = mybir.dt.float32
BF16 = mybir.dt.bfloat16
AF = mybir.ActivationFunctionType
ALU = mybir.AluOpType
ENGS = ("pe", "act", "dve", "pool", "sp")

NCORES = 8
SEQ = 16384
D = 1024
T = SEQ // NCORES
KC = D // 128
FFN_H = 2816
NJ = FFN_H // 128
DILS = (1, 4, 16)
NEG = -30000.0
RMS_EPS = 1e-6
LN_EPS = 1e-5
HW = 32
CW = 31


class Buf:
    __slots__ = ("w", "r", "pw", "pr", "sem", "semval", "name", "psum")

    def __init__(self, name="", psum=False):
        self.psum = psum
        self.w = {}
        self.r = {}
        self.pw = {}
        self.pr = {}
        self.sem = None
        self.semval = 0
        self.name = name


def _merge(dst, src):
    for k, v in src.items():
        if k not in dst or dst[k][1] < v[1]:
            dst[k] = v


class Sched:
    def __init__(self, nc, stack):
        self.nc = nc
        self.stack = stack
        self.sem = {e: stack.enter_context(nc.semaphore("s_" + e)) for e in ENGS}
        self.cnt = {e: 0 for e in ENGS}
        self.seen = {e: {} for e in ENGS}
        self.prog = {e: [] for e in ENGS}
        self.dma_bufs = []
        self.nsem = 0

    def renew(self, b):
        b.pw = dict(b.w)
        b.pr = dict(b.r)
        b.w = {}
        b.r = {}

    def _waits(self, eng, reads, writes, pwrites):
        waits = {}
        seen = self.seen[eng]

        def need(deps, is_reader):
            for key, (sem, val) in deps.items():
                if key[0] == "e" and key[1] == eng and (eng in ("pe", "sp") or is_reader):
                    continue
                if seen.get(key, 0) >= val:
                    continue
                if key not in waits or waits[key][1] < val:
                    waits[key] = (sem, val)

        for b in reads:
            need(b.w, False)
            if b.psum:
                need(b.r, True)
        for b in writes:
            self.renew(b)
        for b in list(writes) + list(pwrites):
            need(b.pw, False)
            need(b.pr, True)
        for b in pwrites:
            if b.psum:
                need(b.r, True)
        for key, (sem, val) in waits.items():
            seen[key] = val
        return list(waits.values())

    def _record(self, key, me, reads, writes, pwrites):
        for b in reads:
            b.r[key] = me
        for b in list(writes) + list(pwrites):
            b.w[key] = me

    def op(self, eng, fn, reads=(), writes=(), pwrites=()):
        waits = self._waits(eng, reads, writes, pwrites)
        self.cnt[eng] += 1
        self.prog[eng].append((waits, fn, (self.sem[eng], 1)))
        self._record(("e", eng), (self.sem[eng], self.cnt[eng]), reads, writes, pwrites)

    def dma(self, eng, fn, sbuf, is_load, reads=(), writes=(), pwrites=(), partial=False):
        reads = list(reads)
        writes = list(writes)
        pwrites = list(pwrites)
        if is_load:
            (pwrites if partial else writes).append(sbuf)
        else:
            reads.append(sbuf)
        waits = self._waits(eng, reads, writes, pwrites)
        if sbuf.sem is None:
            self.nsem += 1
            sbuf.sem = self.stack.enter_context(self.nc.semaphore("d%d" % self.nsem))
            self.dma_bufs.append(sbuf)
        sbuf.semval += 16
        self.prog[eng].append((waits, fn, (sbuf.sem, 16)))
        self._record(("d", id(sbuf)), (sbuf.sem, sbuf.semval), reads, writes, pwrites)

    def barrier(self):
        for e in ENGS:
            waits = []
            for e2 in ENGS:
                if (e2 != e or e in ("act", "dve", "pool")) and self.cnt[e2] > self.seen[e].get(("e", e2), 0):
                    waits.append((self.sem[e2], self.cnt[e2]))
                    self.seen[e][("e", e2)] = self.cnt[e2]
            for b in self.dma_bufs:
                key = ("d", id(b))
                if b.semval > self.seen[e].get(key, 0):
                    waits.append((b.sem, b.semval))
                    self.seen[e][key] = b.semval
            if waits:
                self.prog[e].append((waits, None, None))

    def emit(self):
        nc = self.nc
        self.barrier()
        prog = self.prog

        def run(e, name):
            for waits, fn, inc in prog[name]:
                for sem, val in waits:
                    e.wait_ge(sem, val)
                if fn is not None:
                    ins = fn(e)
                    ins.then_inc(inc[0], inc[1])

        with nc.Block() as block:
            @block.tensor
            def _(e):
                run(e, "pe")

            @block.scalar
            def _(e):
                run(e, "act")

            @block.vector
            def _(e):
                run(e, "dve")

            @block.gpsimd
            def _(e):
                run(e, "pool")

            @block.sync
            def _(e):
                run(e, "sp")


_DTSIZE = {F32: 4, BF16: 2}


class Arena:
    BASE = 16512
    LIMIT = 16512 + 204 * 1024

    def __init__(self, nc):
        self.nc = nc
        self.lo = self.BASE
        self.hi = self.LIMIT
        self.n = 0

    def alloc(self, name, shape, dt, side="L"):
        nb = _DTSIZE[dt]
        for d in shape[1:]:
            nb *= d
        nb = (nb + 63) // 64 * 64
        if side == "L":
            off = self.lo
            self.lo += nb
        else:
            self.hi -= nb
            off = self.hi
        assert self.lo <= self.hi, "SBUF arena overflow at %s: lo=%d hi=%d" % (name, self.lo, self.hi)
        self.n += 1
        return self.nc.alloc_sbuf_tensor_at("%s_%d" % (name, self.n), list(shape), dt, offset=off)


class K:
    def __init__(self, debug=False):
        self.debug = debug
        self.stop = 99
        self.evac_dve_only = False
        self.nc = bass.Bass("TRN2", target_bir_lowering=False)

    def act(self, out, in_, func, reads=(), writes=(), pwrites=(), bias=None, scale=None, accum=None):
        kw = {}
        if bias is not None:
            kw["bias"] = bias
        if scale is not None:
            kw["scale"] = scale
        if accum is not None:
            kw["accum_out"] = accum
        self.S.op("act", lambda e: e.activation(out=out, in_=in_, func=func, **kw), reads, writes, pwrites)

    def ts(self, eng, out, in0, s1, s2, op0, op1=None, reads=(), writes=(), pwrites=()):
        if op1 is None:
            self.S.op(eng, lambda e: e.tensor_scalar(out=out, in0=in0, scalar1=s1, scalar2=None, op0=op0),
                      reads, writes, pwrites)
        else:
            self.S.op(eng, lambda e: e.tensor_scalar(out=out, in0=in0, scalar1=s1, scalar2=s2, op0=op0, op1=op1),
                      reads, writes, pwrites)

    def tt(self, eng, out, in0, in1, op, reads=(), writes=(), pwrites=()):
        self.S.op(eng, lambda e: e.tensor_tensor(out=out, in0=in0, in1=in1, op=op), reads, writes, pwrites)

    def stt(self, out, in0, scalar, in1, op0, op1, reads=(), writes=(), pwrites=()):
        self.S.op("dve", lambda e: e.scalar_tensor_tensor(out=out, in0=in0, scalar=scalar, in1=in1, op0=op0, op1=op1),
                  reads, writes, pwrites)

    def copy(self, eng, out, in_, reads=(), writes=(), pwrites=()):
        self.S.op(eng, lambda e: e.tensor_copy(out=out, in_=in_), reads, writes, pwrites)

    def mm(self, out, pairs, reads=(), writes=(), pwrites=(), start=True, stop=True):
        def fn(e):
            n = len(pairs)
            ins = None
            for i, (l, r) in enumerate(pairs):
                ins = e.matmul(out, l, r, start=(start and i == 0), stop=(stop and i == n - 1))
            return ins
        self.S.op("pe", fn, reads, writes, pwrites)

    def load(self, eng, out, in_, buf, reads=(), partial=False):
        self.S.dma(eng, lambda e: e.dma_start(out=out, in_=in_), buf, True, reads=reads, partial=partial)

    def store(self, eng, out, in_, buf):
        self.S.dma(eng, lambda e: e.dma_start(out=out, in_=in_), buf, False)

    def build(self):
        nc = self.nc
        dram = lambda name, shape, kind="ExternalInput": nc.dram_tensor(name, shape, F32, kind=kind).ap()
        self.x_own = dram("x_own", [T, D])
        self.x_halo = dram("x_halo", [T, D])
        self.cmask_d = dram("cmask", [128, 2])
        self.w_qkv = dram("w_qkv", [12, 128, 3, 1024])
        self.w_glu = dram("w_glu", [8, 128, 2, 1024])
        self.w_gate = dram("w_gate", [8, 128, 2, 1024])
        self.w_ao = dram("w_ao", [128, 4, 1024])
        self.w_co = dram("w_co", [128, 8, 1024])
        self.w_mo = dram("w_mo", [128, 8, 1024])
        self.w_f1 = dram("w_f1", [NJ, 128, 2, 1024])
        self.w_f2 = dram("w_f2", [128, NJ, 1024])
        self.vecT_d = dram("vecT", [128, 64])
        self.wdwT_d = dram("wdwT", [128, 8 * CW])
        self.gbc_d = dram("gbc", [128, 2, 1024])
        self.bias_d = dram("bias", [12, 128, 1024])
        self.y = dram("y", [T, D], kind="ExternalOutput")
        if self.debug:
            self.dbg = {
                "d_hT": nc.dram_tensor("d_hT", [128, KC * (128 + T)], BF16, kind="ExternalOutput").ap(),
                "d_hH": nc.dram_tensor("d_hH", [128, KC * T], BF16, kind="ExternalOutput").ap(),
                "d_oT": nc.dram_tensor("d_oT", [128, 4 * T], BF16, kind="ExternalOutput").ap(),
                "d_sT": nc.dram_tensor("d_sT", [128, KC * T], BF16, kind="ExternalOutput").ap(),
                "d_mT": nc.dram_tensor("d_mT", [128, KC * T], BF16, kind="ExternalOutput").ap(),
                "d_x1": nc.dram_tensor("d_x1", [128, 16 * D], F32, kind="ExternalOutput").ap(),
            }

        with ExitStack() as st0:
            self.S = Sched(nc, st0)
            S = self.S
            A = Arena(nc)
            self.A = A
            pst = [st0.enter_context(nc.psum_tensor("ps%d" % i, [128, 1024], F32)) for i in range(4)]
            self.pst = pst
            self.bank = [pst[k // 2][:, (k % 2) * 512:(k % 2 + 1) * 512] for k in range(8)]
            self.bb = [Buf("bank%d" % k, psum=True) for k in range(8)]

            self.ident = A.alloc("ident", [128, 128], F32)
            self.onesm = A.alloc("onesm", [128, 128], BF16)
            self.onesr = A.alloc("onesr", [128, 64], BF16)
            self.nhalf = A.alloc("nhalf", [128, 512], F32)
            self.vecT = A.alloc("vecT", [128, 8, 8], F32)
            self.wdwT = A.alloc("wdwT", [128, 8, CW], F32)
            self.cmask = A.alloc("cmask", [128, 2], F32)
            self.stat = A.alloc("stat", [128, 4, 64], F32)
            self.b_const = Buf("const")
            self.b_vec, self.b_wdw, self.b_cm = Buf(), Buf(), Buf()
            S.op("pool", lambda e: e.memset(self.ident[:], 0.0), pwrites=[self.b_const])
            S.op("pool", lambda e: e.affine_select(out=self.ident[:], in_=self.ident[:], pattern=[[-1, 128]],
                                                   compare_op=ALU.not_equal, fill=1.0, base=0,
                                                   channel_multiplier=1),
                 reads=[self.b_const], pwrites=[self.b_const])
            S.op("dve", lambda e: e.memset(self.onesm[:], 1.0 / 1024.0), pwrites=[self.b_const])
            S.op("dve", lambda e: e.memset(self.onesr[:], 1.0), pwrites=[self.b_const])
            S.op("dve", lambda e: e.memset(self.nhalf[:], -0.5), pwrites=[self.b_const])
            self.load("sp", self.vecT[:, :, :].rearrange("p a b -> p (a b)"), self.vecT_d, self.b_vec)
            self.load("sp", self.wdwT[:, :, :].rearrange("p a b -> p (a b)"), self.wdwT_d, self.b_wdw)
            self.load("sp", self.cmask[:], self.cmask_d, self.b_cm)
            base_lo = A.lo
            if self.stop == 0:
                S.emit()
                return nc

            self.hT_own = A.alloc("hT_own", [128, KC, 128 + T], BF16)
            self.b_hown = Buf("hT_own")
            self.oT = A.alloc("oT", [128, 4, T], BF16)
            self.b_oT = Buf("oT")
            lo_after_hown = A.lo
            self.hT_halo = A.alloc("hT_halo", [128, KC, T], BF16)
            self.b_hhalo = Buf("hT_halo")
            lo_after_halo = A.lo
            S.renew(self.b_hown), S.renew(self.b_hhalo)
            self.wq = [A.alloc("wq%d" % i, [128, 3, 1024], BF16) for i in range(2)]
            self.b_wq = [Buf() for _ in range(2)]
            self.abias = [A.alloc("bias%d" % i, [128, 2, 512], F32) for i in range(3)]
            self.b_abias = [Buf() for _ in range(3)]
            for u in range(2):
                self.load("pool", self.wq[u][:, :, :], self.w_qkv[u], self.b_wq[u])
                self.load("sp", self.abias[u][:, :, :].rearrange("p a b -> p (a b)"), self.bias_d[u], self.b_abias[u])
            lo_after_p2w = A.lo
            self.phase1()
            if self.debug:
                self.store("sp", self.dbg["d_hT"], self.hT_own[:, :, :].rearrange("p a b -> p (a b)"), self.b_hown)
                self.store("sp", self.dbg["d_hH"], self.hT_halo[:, :, :].rearrange("p a b -> p (a b)"), self.b_hhalo)
            if self.stop == 1:
                S.emit()
                return nc
            S.barrier()
            A.lo = lo_after_p2w
            S.renew(self.b_oT)
            self.phase2()
            if self.debug:
                self.store("sp", self.dbg["d_oT"], self.oT[:, :, :].rearrange("p a b -> p (a b)"), self.b_oT)
            if self.stop == 2:
                S.emit()
                return nc
            S.barrier()
            A.lo = lo_after_hown
            self.sT = A.alloc("sT", [128, KC, T], BF16)
            self.b_sT = [[Buf() for _ in range(4)] for _ in range(KC)]
            self.wao = A.alloc("wao", [128, 4, 1024], BF16)
            self.wco = A.alloc("wco", [128, 8, 1024], BF16)
            self.wgt = [A.alloc("wgt%d" % i, [128, 2, 1024], BF16) for i in range(3)]
            self.b_wao, self.b_wco = Buf(), Buf()
            self.b_wgt = [Buf() for _ in range(3)]
            lo_after_sT = A.lo
            self.phase3()
            if self.debug:
                for c in range(KC):
                    for tt in range(4):
                        self.store("sp", self.dbg["d_sT"][:, c * T + tt * 512:c * T + (tt + 1) * 512],
                                   self.sT[:, c, tt * 512:(tt + 1) * 512], self.b_sT[c][tt])
            if self.stop == 3:
                S.emit()
                return nc
            S.barrier()
            A.lo = lo_after_sT
            self.mT = A.alloc("mT", [128, KC, T], BF16, side="R")
            self.b_mT = Buf("mT")
            S.renew(self.b_mT)
            self.wmo = A.alloc("wmo", [128, 8, 1024], BF16, side="R")
            self.b_wmo = Buf()
            self.load("pool", self.wmo[:, :, :], self.w_mo, self.b_wmo)
            self.phase4()
            if self.debug:
                self.store("sp", self.dbg["d_mT"], self.mT[:, :, :].rearrange("p a b -> p (a b)"), self.b_mT)
            if self.stop == 4:
                S.emit()
                return nc
            S.barrier()
            A.lo = base_lo
            self.x1 = A.alloc("x1", [128, 16, D], F32)
            self.b_x1 = [Buf() for _ in range(16)]
            self.wf1 = [A.alloc("wf1_%d" % i, [128, 2, 1024], BF16) for i in range(4)]
            self.b_wf1 = [Buf() for _ in range(4)]
            self.gbc6 = A.alloc("gbc6", [128, 1024], F32)
            self.b_gbc6 = Buf()
            self.h2T = [A.alloc("h2T%d" % i, [128, KC, 512], BF16) for i in range(2)]
            self.b_h2T = [Buf() for _ in range(2)]
            self.scr6 = self.alloc_norm_scratch("p6", nxs=4)
            lo_after_x1 = A.lo
            self.phase5()
            if self.debug:
                for t in range(16):
                    self.store("sp", self.dbg["d_x1"][:, t * D:(t + 1) * D], self.x1[:, t, :], self.b_x1[t])
            if self.stop == 5:
                S.emit()
                return nc
            S.barrier()
            A.lo = lo_after_x1
            A.hi = A.LIMIT
            self.phase6()
            S.emit()
        return nc

    def norm_stages(self, scr, src_fn, dst_fn, gcol, col0, banks=(0, 1, 2, 3)):
        S = self.S
        junk, xs, b_xs = scr
        nxs = len(xs)
        bt = {}
        ss, ms, rs = self.stat[:, 0, :], self.stat[:, 1, :], self.stat[:, 2, :]

        def stats(t):
            ap, buf = src_fn(t)
            cl = slice(col0 + t, col0 + t + 1)
            bt[t] = (Buf(), Buf(), Buf())
            b_ss, b_ms, b_rs = bt[t]
            self.sq_accum(junk, ap, [buf], b_ss, ss[:, cl])
            self.ts("dve", ms[:, cl], ss[:, cl], 1.0 / D, RMS_EPS, ALU.mult, ALU.add, reads=[b_ss], pwrites=[b_ms])
            self.rsqrt_act(rs[:, cl], ms[:, cl], self.stat[:, 3, cl], b_ms, b_rs)

        def scale(t):
            ap, buf = src_fn(t)
            cl = slice(col0 + t, col0 + t + 1)
            i = t % nxs
            self.act(xs[i][:, :], ap, AF.Identity, reads=[buf, bt[t][2]], writes=[b_xs[i]], scale=rs[:, cl])

        def transpose(t):
            i = t % nxs
            pb = (banks[0], banks[1]) if t % 2 == 0 else (banks[2], banks[3])
            for half in range(2):
                def tr(e, half=half, i=i, pb=pb):
                    ins = None
                    for q in range(4):
                        kc = half * 4 + q
                        ins = e.transpose(self.bank[pb[half]][:, q * 128:(q + 1) * 128],
                                          xs[i][:, kc * 128:(kc + 1) * 128], self.ident[:, :])
                    return ins
                S.op("pe", tr, reads=[b_xs[i], self.b_const], writes=[self.bb[pb[half]]])
            for kc in range(KC):
                srcp = self.bank[pb[kc // 4]][:, (kc % 4) * 128:(kc % 4 + 1) * 128]
                for (dst, dbuf) in dst_fn(t, kc):
                    if kc < 4 or self.evac_dve_only:
                        self.ts("dve", dst, srcp, self.vecT[:, kc, gcol:gcol + 1], None, ALU.mult,
                                reads=[self.bb[pb[kc // 4]], self.b_vec], pwrites=[dbuf])
                    else:
                        self.act(dst, srcp, AF.Identity, reads=[self.bb[pb[kc // 4]], self.b_vec], pwrites=[dbuf],
                                 scale=self.vecT[:, kc, gcol:gcol + 1])

        return stats, scale, transpose

    def norm_transpose(self, scr, src_fn, n_tiles, dst_fn, gcol, col0):
        stats, scale, transpose = self.norm_stages(scr, src_fn, dst_fn, gcol, col0)
        stats(0)
        if n_tiles > 1:
            stats(1)
        scale(0)
        for t in range(n_tiles):
            if t + 2 < n_tiles:
                stats(t + 2)
            if t + 1 < n_tiles:
                scale(t + 1)
            transpose(t)

    def rsqrt_act(self, rs_ap, ms_ap, ln_ap, b_ms, b_rs):
        b_ln = Buf()
        self.act(ln_ap, ms_ap, AF.Ln, reads=[b_ms], writes=[b_ln])
        self.act(rs_ap, ln_ap, AF.Exp, reads=[b_ln], pwrites=[b_rs], scale=-0.5)

    def sq_accum(self, junk, ap, reads, b_ss, ss_ap):
        tiles, bufs, k = junk
        i = k[0] % len(tiles)
        k[0] += 1
        self.act(tiles[i][:, :], ap, AF.Square, reads=reads, writes=[bufs[i]], pwrites=[b_ss], accum=ss_ap)

    def alloc_norm_scratch(self, tag, nxs=2):
        A = self.A
        junk = [A.alloc("junk%s%d" % (tag, i), [128, D], BF16) for i in range(2)]
        xs = [A.alloc("xs%s%d" % (tag, i), [128, D], F32) for i in range(nxs)]
        return (junk, [Buf() for _ in range(2)], [0]), xs, [Buf() for _ in range(nxs)]

    def phase1(self):
        A = self.A
        NS = 4
        xin = [A.alloc("xin%d" % i, [128, D], F32) for i in range(NS)]
        b_xin = [Buf() for _ in range(NS)]
        scr = self.alloc_norm_scratch("p1")
        loaded = {}

        def src(t):
            i = t % NS
            if t not in loaded:
                loaded[t] = True
                d = self.x_halo if t < 16 else self.x_own
                r = (t % 16) * 128
                self.load("sp", xin[i][:, :], d[r:r + 128, :], b_xin[i])
            return xin[i][:, :], b_xin[i]

        def dst(t, kc):
            if t < 16:
                out = [(self.hT_halo[:, kc, t * 128:(t + 1) * 128], self.b_hhalo)]
                if t == 15:
                    out.append((self.hT_own[:, kc, 0:128], self.b_hown))
                return out
            return [(self.hT_own[:, kc, 128 + (t - 16) * 128:128 + (t - 15) * 128], self.b_hown)]

        self.evac_dve_only = True
        self.norm_transpose(scr, src, 32, dst, 0, 0)
        self.evac_dve_only = False

    def phase2(self):
        S = self.S
        A = self.A
        hT_own, hT_halo = self.hT_own, self.hT_halo
        NW = 2
        NB = 3
        wq, b_wq, bias, b_bias = self.wq, self.b_wq, self.abias, self.b_abias
        qT = [A.alloc("qT%d" % i, [128, T], BF16) for i in range(2)]
        kT = [A.alloc("kT%d" % i, [128, 2 * T], BF16) for i in range(2)]
        Vt = [A.alloc("V%d" % i, [128, 32, 2, 128], BF16) for i in range(2)]
        b_q = [Buf() for _ in range(2)]
        b_k = [Buf() for _ in range(2)]
        b_v = [Buf() for _ in range(2)]
        NT = 3
        tmp = [A.alloc("tmp%d" % i, [128, 512], F32) for i in range(NT)]
        b_tmp = [Buf() for _ in range(NT)]
        NP = 6
        Pb = [A.alloc("P%d" % i, [128, 512], BF16) for i in range(NP)]
        b_P = [Buf() for _ in range(NP)]
        acc = A.alloc("acc", [128, 2, T], F32)
        b_acc = Buf()
        lnd = [A.alloc("lnd%d" % i, [64, 512], F32) for i in range(2)]
        rb = [A.alloc("rb%d" % i, [64, 512], F32) for i in range(2)]
        b_lnd = [Buf() for _ in range(2)]
        b_rb = [Buf() for _ in range(2)]
        oT, b_oT = self.oT, self.b_oT
        b_ones = [Buf(), Buf()]
        for i in range(2):
            S.op("pool", lambda e, i=i: e.memset(Vt[i][:, :, :, 64:128], 1.0), writes=[b_ones[i]])

        proj_rr = [0]
        cnt_tmp = [0]
        cnt_P = [0]
        cnt_ev = [0]

        def proj_bank():
            k = proj_rr[0] % 2
            proj_rr[0] += 1
            return k

        def params(u):
            hp, g = divmod(u, 3)
            dl = DILS[g]
            M = T // dl
            nb = M // 128
            return hp, g, u % 2, dl, M, nb, 128 + M

        def loads(u):
            if u >= 12:
                return
            self.load("pool", wq[u % NW][:, :, :], self.w_qkv[u], b_wq[u % NW])
            self.load("sp", bias[u % NB][:, :, :].rearrange("p a b -> p (a b)"), self.bias_d[u], b_bias[u % NB])

        def proj_steps(u):
            hp, g, s, dl, M, nb, KW = params(u)
            w = wq[u % NW]
            bw = b_wq[u % NW]
            steps = []
            qv = qT[s][:, :].rearrange("p (c m) -> p c m", c=dl)
            kv = kT[s][:, 0:dl * KW].rearrange("p (c m) -> p c m", c=dl)

            def evac(dst, srcp, pk, dbuf):
                if cnt_ev[0] % 3 == 2:
                    self.copy("dve", dst, srcp, reads=[self.bb[pk]], pwrites=[dbuf])
                else:
                    self.act(dst, srcp, AF.Identity, reads=[self.bb[pk]], pwrites=[dbuf])
                cnt_ev[0] += 1

            def renew_all():
                S.renew(b_q[s]), S.renew(b_k[s]), S.renew(b_v[s])
            steps.append(renew_all)
            for which, dstv, dbuf, moff in ((0, qv, b_q[s], 0), (1, kv, b_k[s], 128)):
                for tt in range(4):
                    def st(which=which, dstv=dstv, dbuf=dbuf, moff=moff, tt=tt):
                        pk = proj_bank()
                        self.mm(self.bank[pk][:, :],
                                [(w[:, which, kc * 128:(kc + 1) * 128],
                                  hT_own[:, kc, 128 + tt * 512:128 + (tt + 1) * 512]) for kc in range(KC)],
                                reads=[bw, self.b_hown], writes=[self.bb[pk]])
                        m0 = tt * 512 // dl
                        srcp = self.bank[pk][:, :].rearrange("p (m c) -> p c m", c=dl)
                        evac(dstv[:, :, moff + m0:moff + m0 + 512 // dl], srcp, pk, dbuf)
                    steps.append(st)
            nh = 128 * dl
            h0 = T - nh
            for t0 in range(0, nh, 512):
                def st(t0=t0):
                    n = min(512, nh - t0)
                    pk = proj_bank()
                    self.mm(self.bank[pk][:, 0:n],
                            [(w[:, 1, kc * 128:(kc + 1) * 128], hT_halo[:, kc, h0 + t0:h0 + t0 + n])
                             for kc in range(KC)],
                            reads=[bw, self.b_hhalo], writes=[self.bb[pk]])
                    m0 = t0 // dl
                    srcp = self.bank[pk][:, 0:n].rearrange("p (m c) -> p c m", c=dl)
                    evac(kv[:, :, m0:m0 + n // dl], srcp, pk, b_k[s])
                steps.append(st)
            nblk = dl * (nb + 1)
            for b0 in range(0, nblk, 4):
                def st(b0=b0):
                    pk = proj_bank()
                    nq4 = min(4, nblk - b0)

                    def vfn(e):
                        ins = None
                        for q in range(nq4):
                            c, j = divmod(b0 + q, nb + 1)
                            for kc in range(KC):
                                if j == 0:
                                    a0 = h0 + c
                                    lhs = hT_halo[:, kc, a0:a0 + 127 * dl + 1:dl]
                                else:
                                    a0 = 128 + dl * 128 * (j - 1) + c
                                    lhs = hT_own[:, kc, a0:a0 + 127 * dl + 1:dl]
                                ins = e.matmul(self.bank[pk][:, q * 128:(q + 1) * 128], lhs,
                                               w[:, 2, kc * 128:(kc + 1) * 128],
                                               start=(kc == 0), stop=(kc == KC - 1))
                        return ins
                    S.op("pe", vfn, reads=[bw, self.b_hown, self.b_hhalo], writes=[self.bb[pk]])
                    srcp = self.bank[pk][:, 0:nq4 * 128].rearrange("p (b d) -> p b d", d=64)
                    dst = Vt[s][:, b0:b0 + nq4, :, 0:64].rearrange("p b e d -> p (b e) d")
                    evac(dst, srcp, pk, b_v[s])
                steps.append(st)
            return steps

        def attention(u, filler, deferred=None):
            hp, g, s, dl, M, nb, KW = params(u)
            bs = bias[u % NB]
            b_bs = b_bias[u % NB]
            items = []
            for c in range(dl):
                for j0 in range(0, nb + 1, 2):
                    items.append((c, [j for j in (j0, j0 + 1) if j <= nb]))

            def cols_of(j, jj):
                lo_ = jj * 256 + (128 if j == 0 else 0)
                hi_ = jj * 256 + (128 if j == nb else 256)
                return lo_, hi_

            def scores(it, sset):
                c, jl = it
                for e_ in range(2):
                    pk = 2 + sset * 2 + e_

                    def sfn(e, e_=e_, pk=pk):
                        ins = None
                        for jj, j in enumerate(jl):
                            lo_, hi_ = cols_of(j, jj)
                            qb0 = j - 1 if j > 0 else 0
                            nq = (hi_ - lo_) // 128
                            ins = e.matmul(self.bank[pk][:, lo_:hi_],
                                           kT[s][64 * e_:64 * e_ + 64, c * KW + j * 128:c * KW + (j + 1) * 128],
                                           qT[s][64 * e_:64 * e_ + 64, c * M + qb0 * 128:c * M + (qb0 + nq) * 128],
                                           start=True, stop=True)
                        return ins
                    S.op("pe", sfn, reads=[b_q[s], b_k[s]], writes=[self.bb[pk]])

            def softmax(it, sset):
                c, jl = it
                lo_ = cols_of(jl[0], 0)[0]
                hi_ = cols_of(jl[-1], len(jl) - 1)[1]
                res = []
                for e_ in range(2):
                    pk = 2 + sset * 2 + e_
                    ti = cnt_tmp[0] % NT
                    cnt_tmp[0] += 1
                    pi = cnt_P[0] % NP
                    cnt_P[0] += 1
                    self.stt(tmp[ti][:, lo_:hi_], self.bank[pk][:, lo_:hi_], 0.125, bs[:, e_, lo_:hi_],
                             ALU.mult, ALU.add, reads=[self.bb[pk], b_bs], writes=[b_tmp[ti]])
                    if jl[0] == 0:
                        self.act(Pb[pi][:, lo_:lo_ + 128], tmp[ti][:, lo_:lo_ + 128], AF.Exp,
                                 reads=[b_tmp[ti], self.b_cm], writes=[b_P[pi]], bias=self.cmask[:, 1:2])
                        if hi_ > lo_ + 128:
                            self.act(Pb[pi][:, lo_ + 128:hi_], tmp[ti][:, lo_ + 128:hi_], AF.Exp,
                                     reads=[b_tmp[ti]], pwrites=[b_P[pi]])
                    else:
                        self.act(Pb[pi][:, lo_:hi_], tmp[ti][:, lo_:hi_], AF.Exp,
                                 reads=[b_tmp[ti]], writes=[b_P[pi]])
                    res.append(pi)
                return res

            def evac_pv(first_lin, nblocks):
                for e_ in range(2):
                    pk = 6 + e_
                    if nb >= 4:
                        c, b = divmod(first_lin, nb)
                        a0 = dl * 128 * b + c
                        n = 128 * nblocks
                        dst = acc[:, e_, a0:a0 + (n - 1) * dl + 1:dl]
                        srcp = self.bank[pk][:, 0:n]
                    else:
                        c0 = first_lin
                        dst = acc[:, e_, :].rearrange("p (a c) -> p c a", c=dl)[:, c0:c0 + nblocks, :]
                        srcp = self.bank[pk][:, 0:128 * nblocks].rearrange("p (c a) -> p c a", a=128)
                    if g == 0:
                        self.copy("dve", dst, srcp, reads=[self.bb[pk]], pwrites=[b_acc])
                    else:
                        self.tt("dve", dst, dst, srcp, ALU.add, reads=[self.bb[pk], b_acc], pwrites=[b_acc])

            def pv(it, pis):
                c, jl = it
                for jj, j in enumerate(jl):
                    blk = c * (nb + 1) + j
                    roles = []
                    if j > 0:
                        roles.append(("cur", j - 1, jj * 256))
                    if j < nb:
                        roles.append(("prev", j, jj * 256 + 128))
                    for role, b, pcol in roles:
                        lin = c * nb + b
                        slot = lin % 4
                        for e_ in range(2):
                            pk = 6 + e_
                            first = (role == "prev")
                            pi = pis[e_]

                            def pfn(e, pk=pk, slot=slot, blk=blk, e_=e_, pi=pi, pcol=pcol, first=first):
                                return e.matmul(self.bank[pk][:, slot * 128:(slot + 1) * 128],
                                                Vt[s][:, blk, e_, :], Pb[pi][:, pcol:pcol + 128],
                                                start=first, stop=(not first))
                            if first and slot == 0:
                                S.op("pe", pfn, reads=[b_v[s], b_ones[s], b_P[pi]], writes=[self.bb[pk]])
                            else:
                                S.op("pe", pfn, reads=[b_v[s], b_ones[s], b_P[pi]], pwrites=[self.bb[pk]])
                        if role == "cur" and (slot == 3 or lin == dl * nb - 1):
                            evac_pv(lin - slot, slot + 1)

            nfill = len(filler)
            nit = len(items)
            fi = 0
            prev = None
            for idx, it in enumerate(items):
                sset = idx % 2
                scores(it, sset)
                pis = softmax(it, sset)
                tgt = (idx + 1) * nfill // nit
                while fi < tgt:
                    filler[fi]()
                    fi += 1
                if prev is not None:
                    pv(*prev)
                prev = (it, pis)
                if idx == 0:
                    if deferred is not None:
                        deferred()
                    if g == 0:
                        S.renew(b_acc)
            pv(*prev)
            while fi < nfill:
                filler[fi]()
                fi += 1

        def normalize(hp):
            k = 0
            for e_ in range(2):
                for tt in range(4):
                    cs = slice(tt * 512, (tt + 1) * 512)
                    i = k % 2
                    k += 1
                    self.act(lnd[i][:, :], acc[64:128, e_, cs], AF.Ln, reads=[b_acc], writes=[b_lnd[i]])
                    self.act(rb[i][:, :], lnd[i][:, :], AF.Exp, reads=[b_lnd[i]], writes=[b_rb[i]], scale=-1.0)
                    self.tt("dve", oT[64 * e_:64 * e_ + 64, hp, cs], acc[0:64, e_, cs], rb[i][:, :], ALU.mult,
                            reads=[b_acc, b_rb[i]], pwrites=[b_oT])

        for st in proj_steps(0):
            st()
        pending = None
        for u in range(12):
            loads(u + 2)
            filler = proj_steps(u + 1) if u + 1 < 12 else []
            attention(u, filler, pending)
            pending = None
            if u % 3 == 2:
                if u == 11:
                    normalize(u // 3)
                else:
                    pending = (lambda hp=u // 3: normalize(hp))

    def phase3(self):
        S = self.S
        A = self.A
        hT_own = self.hT_own
        sT, b_sT = self.sT, self.b_sT
        uT = A.alloc("uT", [128, KC, HW + T], BF16)
        b_uT = [Buf() for _ in range(KC)]
        lo_wg = A.lo
        wg = [A.alloc("wglu%d" % i, [128, 2, 1024], BF16) for i in range(2)]
        b_wg = [Buf() for _ in range(2)]
        sg = [A.alloc("sg%d" % i, [128, 512], F32) for i in range(2)]
        b_sg = [Buf() for _ in range(2)]
        diag = [A.alloc("diag%d" % i, [128, CW, 128], BF16) for i in range(2)]
        b_dg = [Buf() for _ in range(2)]
        sq = [A.alloc("sq%d" % i, [128, 512], BF16) for i in range(2)]
        b_sq = [Buf() for _ in range(2)]
        m2 = A.alloc("m2", [128, 512], F32)
        veps = A.alloc("veps", [128, 512], F32)
        rstd = A.alloc("rstd", [128, 512], F32)
        b_m2, b_ve, b_rstd = Buf(), Buf(), Buf()
        t1 = [A.alloc("t1_%d" % i, [128, 512], F32) for i in range(2)]
        t2 = [A.alloc("t2_%d" % i, [128, 512], F32) for i in range(2)]
        b_t1 = [Buf() for _ in range(2)]
        b_t2 = [Buf() for _ in range(2)]
        k_sg = 0
        pr = 0
        for c in range(KC):
            s = c % 2
            self.load("pool", wg[s][:, :, :], self.w_glu[c], b_wg[s])
            S.renew(b_uT[c])
            for tt in range(-1, 4):
                if tt < 0:
                    cols = slice(128 - HW, 128)
                    n = HW
                    dst = uT[:, c, 0:HW]
                else:
                    cols = slice(128 + tt * 512, 128 + (tt + 1) * 512)
                    n = 512
                    dst = uT[:, c, HW + tt * 512:HW + (tt + 1) * 512]
                pa, pb_ = pr % 4 * 2, pr % 4 * 2 + 1
                pr += 1
                for which, pk in ((0, pa), (1, pb_)):
                    self.mm(self.bank[pk][:, 0:n],
                            [(wg[s][:, which, kc * 128:(kc + 1) * 128], hT_own[:, kc, cols]) for kc in range(KC)],
                            reads=[b_wg[s], self.b_hown], writes=[self.bb[pk]])
                i = k_sg % 2
                k_sg += 1
                self.act(sg[i][:, 0:n], self.bank[pb_][:, 0:n], AF.Sigmoid, reads=[self.bb[pb_], self.b_vec],
                         writes=[b_sg[i]], bias=self.vecT[:, c, 2:3])
                self.stt(dst, self.bank[pa][:, 0:n], self.vecT[:, c, 1:2], sg[i][:, 0:n], ALU.add, ALU.mult,
                         reads=[self.bb[pa], b_sg[i], self.b_vec], pwrites=[b_uT[c]])
                if tt < 0:
                    self.ts("dve", dst, dst, self.cmask[:, 0:1], None, ALU.mult,
                            reads=[b_uT[c], self.b_cm], pwrites=[b_uT[c]])
        self.load("pool", self.wgt[0][:, :, :], self.w_gate[0], self.b_wgt[0])
        self.load("pool", self.wgt[1][:, :, :], self.w_gate[1], self.b_wgt[1])
        self.load("pool", self.wao[:, :, :], self.w_ao, self.b_wao)
        self.load("pool", self.wco[:, :, :], self.w_co, self.b_wco)
        pr = 0
        for c in range(KC):
            s = c % 2
            S.renew(b_dg[s])
            for j in range(CW):
                if j % 2 == 0:
                    self.ts("dve", diag[s][:, j, :], self.ident[:, :], self.wdwT[:, c, j:j + 1], None, ALU.mult,
                            reads=[self.b_const, self.b_wdw], pwrites=[b_dg[s]])
                else:
                    self.act(diag[s][:, j, :], self.ident[:, :], AF.Identity, reads=[self.b_const, self.b_wdw],
                             pwrites=[b_dg[s]], scale=self.wdwT[:, c, j:j + 1])
            for tt in range(4):
                pk = pr % 4
                pr += 1
                base = tt * 512 + HW - (CW - 1)
                self.mm(self.bank[pk][:, :],
                        [(diag[s][:, j, :], uT[:, c, base + j:base + j + 512]) for j in range(CW)],
                        reads=[b_dg[s], b_uT[c]], writes=[self.bb[pk]])
                self.act(sT[:, c, tt * 512:(tt + 1) * 512], self.bank[pk][:, :], AF.Identity,
                         reads=[self.bb[pk], self.b_vec], writes=[b_sT[c][tt]], bias=self.vecT[:, c, 3:4])

    def phase4(self):
        S = self.S
        A = self.A
        hT_own, oT, b_oT, sT, b_sT, mT = self.hT_own, self.oT, self.b_oT, self.sT, self.b_sT, self.mT
        wao, wco, b_wao, b_wco, wgt, b_wgt = self.wao, self.wco, self.b_wao, self.b_wco, self.wgt, self.b_wgt
        sga = [A.alloc("sga%d" % i, [128, 512], F32) for i in range(2)]
        sgc = [A.alloc("sgc%d" % i, [128, 512], F32) for i in range(2)]
        b_sga = [Buf() for _ in range(2)]
        b_sgc = [Buf() for _ in range(2)]
        sq = [A.alloc("sq%d" % i, [128, 512], BF16) for i in range(2)]
        b_sq = [Buf() for _ in range(2)]
        m2b = [A.alloc("m2_0", [128, 512], F32)] * 2
        vepsb = [A.alloc("veps0", [128, 512], F32)] * 2
        lnv = [A.alloc("lnv0", [128, 512], F32)] * 2
        rstdb = [A.alloc("rstd%d" % i, [128, 512], F32) for i in range(2)]
        nmrb = [A.alloc("nmr%d" % i, [128, 512], F32) for i in range(2)]
        b_m2s = [Buf()] * 2
        b_ves = [Buf()] * 2
        b_lnv = [Buf()] * 2
        b_rstds = [Buf() for _ in range(2)]
        b_nmrs = [Buf() for _ in range(2)]
        t1 = [A.alloc("t1_%d" % i, [128, 512], F32) for i in range(2)]
        b_t1 = [Buf() for _ in range(2)]
        cnt = {"sq": 0, "t": 0}

        def ln_a(tt):
            cs = slice(tt * 512, (tt + 1) * 512)
            pm, pe2 = 0, 1
            S.renew(self.bb[pm]), S.renew(self.bb[pe2])
            for c in range(KC):
                i = cnt["sq"] % 2
                cnt["sq"] += 1
                self.act(sq[i][:, :], sT[:, c, cs], AF.Square, reads=[b_sT[c][tt]], writes=[b_sq[i]])
                self.mm(self.bank[pm][:, :], [(self.onesm[:, :], sT[:, c, cs])],
                        reads=[self.b_const, b_sT[c][tt]], pwrites=[self.bb[pm]],
                        start=(c == 0), stop=(c == KC - 1))
                self.mm(self.bank[pe2][:, :], [(self.onesm[:, :], sq[i][:, :])],
                        reads=[self.b_const, b_sq[i]], pwrites=[self.bb[pe2]],
                        start=(c == 0), stop=(c == KC - 1))

        def ln_b(tt):
            k = tt % 2
            pm, pe2 = 0, 1
            self.act(m2b[k][:, :], self.bank[pm][:, :], AF.Square, reads=[self.bb[pm]], writes=[b_m2s[k]])
            self.stt(vepsb[k][:, :], self.bank[pe2][:, :], LN_EPS, m2b[k][:, :], ALU.add, ALU.subtract,
                     reads=[self.bb[pe2], b_m2s[k]], writes=[b_ves[k]])
            self.act(lnv[k][:, :], vepsb[k][:, :], AF.Ln, reads=[b_ves[k]], writes=[b_lnv[k]])
            self.act(rstdb[k][:, :], lnv[k][:, :], AF.Exp, reads=[b_lnv[k]], writes=[b_rstds[k]], scale=-0.5)
            self.stt(nmrb[k][:, :], self.bank[pm][:, :], -1.0, rstdb[k][:, :], ALU.mult, ALU.mult,
                     reads=[self.bb[pm], b_rstds[k]], writes=[b_nmrs[k]])

        def ln_c(tt, c):
            k = tt % 2
            cs = slice(tt * 512, (tt + 1) * 512)
            i = cnt["t"] % 2
            cnt["t"] += 1
            self.tt("dve", t1[i][:, :], sT[:, c, cs], rstdb[k][:, :], ALU.mult,
                    reads=[b_sT[c][tt], b_rstds[k]], writes=[b_t1[i]])
            self.tt("dve", t1[i][:, :], t1[i][:, :], nmrb[k][:, :], ALU.add,
                    reads=[b_nmrs[k]], writes=[b_t1[i]])
            self.act(sT[:, c, cs], t1[i][:, :], AF.Silu, reads=[b_t1[i], self.b_vec], writes=[b_sT[c][tt]],
                     scale=self.vecT[:, c, 4:5], bias=self.vecT[:, c, 5:6])

        NWG = len(wgt)
        steps = [(tt, c) for tt in range(4) for c in range(KC)]

        def wload(n):
            if 2 <= n < len(steps):
                c = steps[n][1]
                self.load("pool", wgt[n % NWG][:, :, :], self.w_gate[c], b_wgt[n % NWG])

        ln_a(0)
        ln_b(0)
        ln_a(1)
        ln_b(1)
        for c in range(KC):
            ln_c(0, c)
        wload(2)
        def gates_mm(n):
            tt, c = steps[n]
            s = n % NWG
            cs = slice(tt * 512, (tt + 1) * 512)
            hs = slice(128 + tt * 512, 128 + (tt + 1) * 512)
            p0 = (n % 2) * 4
            pga, pgc, pya = p0, p0 + 1, p0 + 2
            self.mm(self.bank[pga][:, :],
                    [(wgt[s][:, 0, kc * 128:(kc + 1) * 128], hT_own[:, kc, hs]) for kc in range(KC)],
                    reads=[b_wgt[s], self.b_hown], writes=[self.bb[pga]])
            self.mm(self.bank[pgc][:, :],
                    [(wgt[s][:, 1, kc * 128:(kc + 1) * 128], hT_own[:, kc, hs]) for kc in range(KC)],
                    reads=[b_wgt[s], self.b_hown], writes=[self.bb[pgc]])
            self.mm(self.bank[pya][:, :],
                    [(wao[:, kc, c * 128:(c + 1) * 128], oT[:, kc, cs]) for kc in range(4)],
                    reads=[b_wao, b_oT], writes=[self.bb[pya]])

        gates_mm(0)
        for n, (tt, c) in enumerate(steps):
            wload(n + 2) if n >= 1 else None
            s = n % NWG
            cs = slice(tt * 512, (tt + 1) * 512)
            hs = slice(128 + tt * 512, 128 + (tt + 1) * 512)
            i = n % 2
            p0 = (n % 2) * 4
            pga, pgc, pya, pyc = p0, p0 + 1, p0 + 2, p0 + 3
            self.mm(self.bank[pyc][:, :],
                    [(wco[:, kc, c * 128:(c + 1) * 128], sT[:, kc, cs]) for kc in range(KC)],
                    reads=[b_wco] + [b_sT[kc][tt] for kc in range(KC)], writes=[self.bb[pyc]])
            if c == KC - 1 and tt + 2 < 4:
                ln_a(tt + 2)
                ln_b(tt + 2)
            if n + 1 < len(steps):
                gates_mm(n + 1)
            self.act(sga[i][:, :], self.bank[pga][:, :], AF.Sigmoid, reads=[self.bb[pga]], writes=[b_sga[i]])
            self.act(sgc[i][:, :], self.bank[pgc][:, :], AF.Sigmoid, reads=[self.bb[pgc]], writes=[b_sgc[i]])
            self.tt("dve", sga[i][:, :], self.bank[pya][:, :], sga[i][:, :], ALU.mult,
                    reads=[self.bb[pya]], writes=[b_sga[i]])
            self.stt(sgc[i][:, :], self.bank[pyc][:, :], self.vecT[:, c, 6:7], sgc[i][:, :], ALU.add, ALU.mult,
                     reads=[self.bb[pyc], self.b_vec], writes=[b_sgc[i]])
            self.tt("dve", mT[:, c, cs], sga[i][:, :], sgc[i][:, :], ALU.add,
                    reads=[b_sga[i], b_sgc[i]], pwrites=[self.b_mT])
            if tt + 1 < 4 and c in (1, 5):
                for cc in range(c - 1, c + 3):
                    ln_c(tt + 1, cc)

    def phase5(self):
        S = self.S
        A = self.A
        x1, b_x1, mT = self.x1, self.b_x1, self.mT
        wmo, b_wmo = self.wmo, self.b_wmo
        for j in range(4):
            self.load("pool", self.wf1[j][:, :, :], self.w_f1[j], self.b_wf1[j])
        self.load("sp", self.gbc6[:, :], self.gbc_d[:, 1, :], self.b_gbc6)
        gbc = A.alloc("gbc5", [128, 1024], F32)
        b_gbc = Buf()
        self.load("sp", gbc[:, :], self.gbc_d[:, 0, :], b_gbc)
        xin = [A.alloc("x5in%d" % i, [128, D], F32) for i in range(2)]
        b_xin = [Buf() for _ in range(2)]
        t1 = [A.alloc("t5_%d" % i, [128, D], F32) for i in range(2)]
        b_t1 = [Buf() for _ in range(2)]
        junk = ([A.alloc("junk5_%d" % i, [128, D], BF16) for i in range(2)], [Buf() for _ in range(2)], [0])
        b_ss, b_ms, b_rs = Buf(), Buf(), Buf()
        ss, ms, rs = self.stat[:, 0, :], self.stat[:, 1, :], self.stat[:, 2, :]
        bt5 = {}

        def front(t):
            i = t % 2
            pp = t % 4
            pa, pb_ = 2 * pp, 2 * pp + 1
            ts_ = slice(t * 128, (t + 1) * 128)
            cl = slice(32 + t, 33 + t)
            self.load("sp", xin[i][:, :], self.x_own[ts_, :], b_xin[i])
            for half, pk in ((0, pa), (1, pb_)):
                self.mm(self.bank[pk][:, :],
                        [(mT[:, kc, ts_], wmo[:, kc, half * 512:(half + 1) * 512]) for kc in range(KC)],
                        reads=[self.b_mT, b_wmo], writes=[self.bb[pk]])
            yps = self.pst[pp][:, :]
            bt5[t] = (Buf(), Buf(), Buf())
            b_ss, b_ms, b_rs = bt5[t]
            self.sq_accum(junk, yps, [self.bb[pa], self.bb[pb_]], b_ss, ss[:, cl])
            self.ts("dve", ms[:, cl], ss[:, cl], 1.0 / D, RMS_EPS, ALU.mult, ALU.add, reads=[b_ss], pwrites=[b_ms])
            self.rsqrt_act(rs[:, cl], ms[:, cl], self.stat[:, 3, cl], b_ms, b_rs)

        def back(t):
            b_rs = bt5[t][2]
            i = t % 2
            pp = t % 4
            pa, pb_ = 2 * pp, 2 * pp + 1
            cl = slice(32 + t, 33 + t)
            yps = self.pst[pp][:, :]
            self.stt(t1[i][:, :], yps, rs[:, cl], gbc[:, :], ALU.mult, ALU.mult,
                     reads=[self.bb[pa], self.bb[pb_], b_rs, b_gbc], writes=[b_t1[i]])
            self.tt("dve", x1[:, t, :], t1[i][:, :], xin[i][:, :], ALU.add,
                    reads=[b_t1[i], b_xin[i]], writes=[b_x1[t]])

        front(0)
        for t in range(16):
            if t + 1 < 16:
                front(t + 1)
            back(t)
        S.renew(self.b_h2T[0])
        self.norm_transpose(self.scr6, lambda i: (x1[:, i, :], b_x1[i]), 4,
                            lambda i, kc: [(self.h2T[0][:, kc, i * 128:(i + 1) * 128], self.b_h2T[0])], 7, 0)

    def phase6(self):
        S = self.S
        A = self.A
        x1, b_x1 = self.x1, self.b_x1
        wf1, b_wf1, gbc, b_gbc, h2T, b_h2T, scr = (self.wf1, self.b_wf1, self.gbc6, self.b_gbc6, self.h2T,
                                                      self.b_h2T, self.scr6)
        wf2 = A.alloc("wf2", [128, NJ, 1024], BF16)
        b_wf2 = Buf()
        aT = A.alloc("aT", [128, NJ, 512], BF16)
        b_aT = [Buf() for _ in range(NJ)]
        sgf = [A.alloc("sgf%d" % i, [128, 512], F32) for i in range(2)]
        b_sgf = [Buf() for _ in range(2)]
        t1 = [A.alloc("t6_%d" % i, [128, D], F32) for i in range(2)]
        b_t1 = [Buf() for _ in range(2)]
        junk = scr[0]
        b_ss, b_ms, b_rs = Buf(), Buf(), Buf()
        ss, ms, rs = self.stat[:, 0, :], self.stat[:, 1, :], self.stat[:, 2, :]
        NWF = len(wf1)
        wf2_pieces = [(0, 6), (6, 12), (12, 17), (17, 22)]
        kj = 0
        for qt in range(4):
            hq = h2T[qt % 2]
            b_hq = b_h2T[qt % 2]
            for j in range(NJ):
                s = kj % NWF
                if kj >= 4:
                    self.load("pool", wf1[s][:, :, :], self.w_f1[j], b_wf1[s])
                if qt == 0 and j < 4:
                    a, b = wf2_pieces[j]
                    self.load("pool", wf2[:, a:b, :], self.w_f2[:, a:b, :], b_wf2, partial=True)
                kj += 1
                pg, pu = 4 + 2 * (j % 2), 5 + 2 * (j % 2)
                for which, pk in ((0, pg), (1, pu)):
                    self.mm(self.bank[pk][:, :],
                            [(wf1[s][:, which, kc * 128:(kc + 1) * 128], hq[:, kc, :]) for kc in range(KC)],
                            reads=[b_wf1[s], b_hq], writes=[self.bb[pk]])
                i = j % 2
                self.act(sgf[i][:, :], self.bank[pg][:, :], AF.Silu, reads=[self.bb[pg]], writes=[b_sgf[i]])
                self.tt("dve", aT[:, j, :], sgf[i][:, :], self.bank[pu][:, :], ALU.mult,
                        reads=[b_sgf[i], self.bb[pu]], writes=[b_aT[j]])
            nxt = None
            if qt + 1 < 4:
                hn = h2T[(qt + 1) % 2]
                b_hn = b_h2T[(qt + 1) % 2]
                S.renew(b_hn)
                nxt = self.norm_stages(
                    scr, lambda i, qt=qt: (x1[:, (qt + 1) * 4 + i, :], b_x1[(qt + 1) * 4 + i]),
                    lambda i, kc, hn=hn, b_hn=b_hn: [(hn[:, kc, i * 128:(i + 1) * 128], b_hn)], 7, (qt + 1) * 4,
                    banks=(4, 5, 6, 7))
                for i4 in range(4):
                    nxt[0](i4)
                for i4 in range(4):
                    nxt[1](i4)
            for i4 in range(4):
                t = qt * 4 + i4
                i = t % 2
                pp = t % 2
                pa, pb_ = 2 * pp, 2 * pp + 1
                cl = slice(48 + t, 49 + t)
                for half, pk in ((0, pa), (1, pb_)):
                    self.mm(self.bank[pk][:, :],
                            [(aT[:, j, i4 * 128:(i4 + 1) * 128], wf2[:, j, half * 512:(half + 1) * 512])
                             for j in range(NJ)],
                            reads=b_aT + [b_wf2], writes=[self.bb[pk]])
                if nxt is not None:
                    nxt[2](i4)
                yps = self.pst[pp][:, :]
                b_ss, b_ms, b_rs = Buf(), Buf(), Buf()
                self.sq_accum(junk, yps, [self.bb[pa], self.bb[pb_]], b_ss, ss[:, cl])
                self.ts("dve", ms[:, cl], ss[:, cl], 1.0 / D, RMS_EPS, ALU.mult, ALU.add, reads=[b_ss], pwrites=[b_ms])
                self.rsqrt_act(rs[:, cl], ms[:, cl], self.stat[:, 3, cl], b_ms, b_rs)
                self.stt(t1[i][:, :], yps, rs[:, cl], gbc[:, :], ALU.mult, ALU.mult,
                         reads=[self.bb[pa], self.bb[pb_], b_rs, b_gbc], writes=[b_t1[i]])
                self.tt("dve", x1[:, t, :], t1[i][:, :], x1[:, t, :], ALU.add,
                        reads=[b_t1[i]], writes=[b_x1[t]])
                self.store("sp", self.y[t * 128:(t + 1) * 128, :], x1[:, t, :], b_x1[t])


def _rel_bucket_np(d):
    d = np.maximum(d, 0)
    df = np.maximum(d, 1).astype(np.float32)
    large = 16 + (np.log(df / np.float32(16)) / np.float32(math.log(2048 / 16)) * np.float32(16)).astype(np.int32)
    large = np.minimum(large, 31)
    return np.where(d < 16, d, large)


def _blk(w, cols):
    n = len(cols)
    return np.ascontiguousarray(w[:, cols].reshape(KC, 128, n).transpose(1, 0, 2)).reshape(128, KC * n)


def _prep_shared(rel_bias_table, g_pre_mix, w_in, b_glu, w_dw, b_dw, g_conv_ln, b_conv_ln, w_conv_out,
                 b_conv_out, w_attn_out, w_mix_out, g_post_mix, g_pre_ffn, w_ffn_in, w_ffn_out, g_post_ffn):
    f = lambda a: np.asarray(a, dtype=np.float32)
    tab = f(rel_bias_table)
    w_in = f(w_in)[0]
    ar = np.arange(128)
    w_qkv = np.empty((12, 128, 3, 1024), np.float32)
    bias = np.empty((12, 128, 2, 4, 128), np.float32)
    kk = ar[:, None]
    qq = ar[None, :]
    for hp in range(4):
        for g in range(3):
            u = hp * 3 + g
            dl = DILS[g]
            for s in range(3):
                w_qkv[u, :, s, :] = _blk(w_in, s * 1536 + g * 512 + hp * 128 + ar)
            idx_cur = _rel_bucket_np((qq - kk) * dl)
            idx_prev = _rel_bucket_np((qq - kk + 128) * dl)
            for e in range(2):
                col = tab[:, g * 8 + hp * 2 + e]
                cur = np.where(qq >= kk, col[idx_cur], np.float32(NEG)).astype(np.float32)
                prev = np.where(kk >= qq, col[idx_prev], np.float32(NEG)).astype(np.float32)
                bias[u, :, e, 0] = cur
                bias[u, :, e, 1] = prev
                bias[u, :, e, 2] = cur
                bias[u, :, e, 3] = prev
    w_glu = np.empty((8, 128, 2, 1024), np.float32)
    w_gate = np.empty((8, 128, 2, 1024), np.float32)
    for c in range(8):
        w_glu[c, :, 0, :] = _blk(w_in, 4608 + c * 128 + ar)
        w_glu[c, :, 1, :] = _blk(w_in, 4608 + 1024 + c * 128 + ar)
        w_gate[c, :, 0, :] = _blk(w_in, 6656 + c * 128 + ar)
        w_gate[c, :, 1, :] = _blk(w_in, 7680 + c * 128 + ar)
    w_f1s = f(w_ffn_in)[0]
    w_f1 = np.empty((NJ, 128, 2, 1024), np.float32)
    for j in range(NJ):
        w_f1[j, :, 0, :] = _blk(w_f1s, j * 128 + ar)
        w_f1[j, :, 1, :] = _blk(w_f1s, FFN_H + j * 128 + ar)
    nat = lambda w, nk: np.ascontiguousarray(f(w)[0].reshape(nk, 128, -1).transpose(1, 0, 2))
    vecs = [f(g_pre_mix)[0], f(b_glu)[0][:1024], f(b_glu)[0][1024:], f(b_dw)[0], f(g_conv_ln)[0], f(b_conv_ln)[0],
            f(b_conv_out)[0], f(g_pre_ffn)[0]]
    vecT = np.ascontiguousarray(np.stack(vecs, axis=0).reshape(8, 8, 128).transpose(2, 1, 0)).reshape(128, 64)
    wdwT = np.ascontiguousarray(f(w_dw)[0].reshape(CW, 8, 128).transpose(2, 1, 0)).reshape(128, 8 * CW)
    gbc = np.ascontiguousarray(np.broadcast_to(np.stack([f(g_post_mix)[0], f(g_post_ffn)[0]])[None], (128, 2, 1024)))
    return {
        "w_qkv": w_qkv, "w_glu": w_glu, "w_gate": w_gate,
        "w_ao": nat(w_attn_out, 4), "w_co": nat(w_conv_out, 8), "w_mo": nat(w_mix_out, 8),
        "w_f1": w_f1, "w_f2": nat(w_ffn_out, NJ),
        "vecT": vecT, "wdwT": wdwT, "gbc": gbc, "bias": bias.reshape(12, 128, 1024),
    }


_NC_CACHE = {}


def _get_nc(debug):
    if debug not in _NC_CACHE:
        _NC_CACHE[debug] = K(debug=debug).build()
    return _NC_CACHE[debug]


def make_in_maps(x, shared, cores):
    x2 = np.asarray(x, dtype=np.float32).reshape(SEQ, D)
    in_maps = []
    for i in cores:
        m = dict(shared)
        m["x_own"] = np.ascontiguousarray(x2[i * T:(i + 1) * T])
        if i == 0:
            m["x_halo"] = np.zeros((T, D), np.float32)
            cm = np.array([0.0, NEG], np.float32)
        else:
            m["x_halo"] = np.ascontiguousarray(x2[(i - 1) * T:i * T])
            cm = np.array([1.0, 0.0], np.float32)
        m["cmask"] = np.ascontiguousarray(np.broadcast_to(cm[None, :], (128, 2)))
        in_maps.append(m)
    return in_maps


def kernel(x, rel_bias_table, g_pre_mix, w_in, b_glu, w_dw, b_dw, g_conv_ln, b_conv_ln, w_conv_out, b_conv_out,
           w_attn_out, w_mix_out, g_post_mix, g_pre_ffn, w_ffn_in, w_ffn_out, g_post_ffn):
    shared = _prep_shared(rel_bias_table, g_pre_mix, w_in, b_glu, w_dw, b_dw, g_conv_ln, b_conv_ln, w_conv_out,
                          b_conv_out, w_attn_out, w_mix_out, g_post_mix, g_pre_ffn, w_ffn_in, w_ffn_out, g_post_ffn)
    nc = _get_nc(False)
    in_maps = make_in_maps(x, shared, list(range(NCORES)))
    res = run_bass_kernel_spmd(nc, in_maps, core_ids=list(range(NCORES)))
    out = np.concatenate([np.asarray(r["y"], dtype=np.float32) for r in res.results], axis=0)
    return out.reshape(1, SEQ, D)
```

```python
import math
import os
from contextlib import ExitStack

import numpy as np
import concourse.bass as bass
import concourse.mybir as mybir
from concourse.bass_utils import run_bass_kernel_spmd

F32 = mybir.dt.float32
BF16 = mybir.dt.bfloat16
AF = mybir.ActivationFunctionType
ALU = mybir.AluOpType
ENGS = ("pe", "act", "dve", "pool", "sp")

NCORES = 8
SEQ = 16384
D = 1024
T = SEQ // NCORES
KC = D // 128
FFN_H = 2816
NJ = FFN_H // 128
DILS = (1, 4, 16)
NEG = -30000.0
RMS_EPS = 1e-6
LN_EPS = 1e-5
HW = 32
CW = 31


class Buf:
    __slots__ = ("w", "r", "pw", "pr", "sem", "semval", "name", "psum")

    def __init__(self, name="", psum=False):
        self.psum = psum
        self.w = {}
        self.r = {}
        self.pw = {}
        self.pr = {}
        self.sem = None
        self.semval = 0
        self.name = name


def _merge(dst, src):
    for k, v in src.items():
        if k not in dst or dst[k][1] < v[1]:
            dst[k] = v


class Sched:
    def __init__(self, nc, stack):
        self.nc = nc
        self.stack = stack
        self.sem = {e: stack.enter_context(nc.semaphore("s_" + e)) for e in ENGS}
        self.cnt = {e: 0 for e in ENGS}
        self.seen = {e: {} for e in ENGS}
        self.prog = {e: [] for e in ENGS}
        self.dma_bufs = []
        self.nsem = 0

    def renew(self, b):
        b.pw = dict(b.w)
        b.pr = dict(b.r)
        b.w = {}
        b.r = {}

    def _waits(self, eng, reads, writes, pwrites):
        waits = {}
        seen = self.seen[eng]

        def need(deps, is_reader):
            for key, (sem, val) in deps.items():
                if key[0] == "e" and key[1] == eng and (eng in ("pe", "sp") or is_reader):
                    continue
                if seen.get(key, 0) >= val:
                    continue
                if key not in waits or waits[key][1] < val:
                    waits[key] = (sem, val)

        for b in reads:
            need(b.w, False)
            if b.psum:
                need(b.r, True)
        for b in writes:
            self.renew(b)
        for b in list(writes) + list(pwrites):
            need(b.pw, False)
            need(b.pr, True)
        for b in pwrites:
            if b.psum:
                need(b.r, True)
        for key, (sem, val) in waits.items():
            seen[key] = val
        return list(waits.values())

    def _record(self, key, me, reads, writes, pwrites):
        for b in reads:
            b.r[key] = me
        for b in list(writes) + list(pwrites):
            b.w[key] = me

    def op(self, eng, fn, reads=(), writes=(), pwrites=()):
        waits = self._waits(eng, reads, writes, pwrites)
        self.cnt[eng] += 1
        self.prog[eng].append((waits, fn, (self.sem[eng], 1)))
        self._record(("e", eng), (self.sem[eng], self.cnt[eng]), reads, writes, pwrites)

    def dma(self, eng, fn, sbuf, is_load, reads=(), writes=(), pwrites=(), partial=False):
        reads = list(reads)
        writes = list(writes)
        pwrites = list(pwrites)
        if is_load:
            (pwrites if partial else writes).append(sbuf)
        else:
            reads.append(sbuf)
        waits = self._waits(eng, reads, writes, pwrites)
        if sbuf.sem is None:
            self.nsem += 1
            sbuf.sem = self.stack.enter_context(self.nc.semaphore("d%d" % self.nsem))
            self.dma_bufs.append(sbuf)
        sbuf.semval += 16
        self.prog[eng].append((waits, fn, (sbuf.sem, 16)))
        self._record(("d", id(sbuf)), (sbuf.sem, sbuf.semval), reads, writes, pwrites)

    def barrier(self):
        for e in ENGS:
            waits = []
            for e2 in ENGS:
                if (e2 != e or e in ("act", "dve", "pool")) and self.cnt[e2] > self.seen[e].get(("e", e2), 0):
                    waits.append((self.sem[e2], self.cnt[e2]))
                    self.seen[e][("e", e2)] = self.cnt[e2]
            for b in self.dma_bufs:
                key = ("d", id(b))
                if b.semval > self.seen[e].get(key, 0):
                    waits.append((b.sem, b.semval))
                    self.seen[e][key] = b.semval
            if waits:
                self.prog[e].append((waits, None, None))

    def emit(self):
        nc = self.nc
        self.barrier()
        prog = self.prog

        def run(e, name):
            for waits, fn, inc in prog[name]:
                for sem, val in waits:
                    e.wait_ge(sem, val)
                if fn is not None:
                    ins = fn(e)
                    ins.then_inc(inc[0], inc[1])

        with nc.Block() as block:
            @block.tensor
            def _(e):
                run(e, "pe")

            @block.scalar
            def _(e):
                run(e, "act")

            @block.vector
            def _(e):
                run(e, "dve")

            @block.gpsimd
            def _(e):
                run(e, "pool")

            @block.sync
            def _(e):
                run(e, "sp")


_DTSIZE = {F32: 4, BF16: 2}


class Arena:
    BASE = 16512
    LIMIT = 16512 + 204 * 1024

    def __init__(self, nc):
        self.nc = nc
        self.lo = self.BASE
        self.hi = self.LIMIT
        self.n = 0

    def alloc(self, name, shape, dt, side="L"):
        nb = _DTSIZE[dt]
        for d in shape[1:]:
            nb *= d
        nb = (nb + 63) // 64 * 64
        if side == "L":
            off = self.lo
            self.lo += nb
        else:
            self.hi -= nb
            off = self.hi
        assert self.lo <= self.hi, "SBUF arena overflow at %s: lo=%d hi=%d" % (name, self.lo, self.hi)
        self.n += 1
        return self.nc.alloc_sbuf_tensor_at("%s_%d" % (name, self.n), list(shape), dt, offset=off)


class K:
    def __init__(self, debug=False):
        self.debug = debug
        self.stop = 99
        self.evac_dve_only = False
        self.nc = bass.Bass("TRN2", target_bir_lowering=False)

    def act(self, out, in_, func, reads=(), writes=(), pwrites=(), bias=None, scale=None, accum=None):
        kw = {}
        if bias is not None:
            kw["bias"] = bias
        if scale is not None:
            kw["scale"] = scale
        if accum is not None:
            kw["accum_out"] = accum
        self.S.op("act", lambda e: e.activation(out=out, in_=in_, func=func, **kw), reads, writes, pwrites)

    def ts(self, eng, out, in0, s1, s2, op0, op1=None, reads=(), writes=(), pwrites=()):
        if op1 is None:
            self.S.op(eng, lambda e: e.tensor_scalar(out=out, in0=in0, scalar1=s1, scalar2=None, op0=op0),
                      reads, writes, pwrites)
        else:
            self.S.op(eng, lambda e: e.tensor_scalar(out=out, in0=in0, scalar1=s1, scalar2=s2, op0=op0, op1=op1),
                      reads, writes, pwrites)

    def tt(self, eng, out, in0, in1, op, reads=(), writes=(), pwrites=()):
        self.S.op(eng, lambda e: e.tensor_tensor(out=out, in0=in0, in1=in1, op=op), reads, writes, pwrites)

    def stt(self, out, in0, scalar, in1, op0, op1, reads=(), writes=(), pwrites=()):
        self.S.op("dve", lambda e: e.scalar_tensor_tensor(out=out, in0=in0, scalar=scalar, in1=in1, op0=op0, op1=op1),
                  reads, writes, pwrites)

    def copy(self, eng, out, in_, reads=(), writes=(), pwrites=()):
        self.S.op(eng, lambda e: e.tensor_copy(out=out, in_=in_), reads, writes, pwrites)

    def mm(self, out, pairs, reads=(), writes=(), pwrites=(), start=True, stop=True):
        def fn(e):
            n = len(pairs)
            ins = None
            for i, (l, r) in enumerate(pairs):
                ins = e.matmul(out, l, r, start=(start and i == 0), stop=(stop and i == n - 1))
            return ins
        self.S.op("pe", fn, reads, writes, pwrites)

    def load(self, eng, out, in_, buf, reads=(), partial=False):
        self.S.dma(eng, lambda e: e.dma_start(out=out, in_=in_), buf, True, reads=reads, partial=partial)

    def store(self, eng, out, in_, buf):
        self.S.dma(eng, lambda e: e.dma_start(out=out, in_=in_), buf, False)

    def build(self):
        nc = self.nc
        dram = lambda name, shape, kind="ExternalInput": nc.dram_tensor(name, shape, F32, kind=kind).ap()
        self.x_own = dram("x_own", [T, D])
        self.x_halo = dram("x_halo", [T, D])
        self.cmask_d = dram("cmask", [128, 2])
        self.w_qkv = dram("w_qkv", [12, 128, 3, 1024])
        self.w_glu = dram("w_glu", [8, 128, 2, 1024])
        self.w_gate = dram("w_gate", [8, 128, 2, 1024])
        self.w_ao = dram("w_ao", [128, 4, 1024])
        self.w_co = dram("w_co", [128, 8, 1024])
        self.w_mo = dram("w_mo", [128, 8, 1024])
        self.w_f1 = dram("w_f1", [NJ, 128, 2, 1024])
        self.w_f2 = dram("w_f2", [128, NJ, 1024])
        self.vecT_d = dram("vecT", [128, 64])
        self.wdwT_d = dram("wdwT", [128, 8 * CW])
        self.gbc_d = dram("gbc", [128, 2, 1024])
        self.bias_d = dram("bias", [12, 128, 1024])
        self.y = dram("y", [T, D], kind="ExternalOutput")
        if self.debug:
            self.dbg = {
                "d_hT": nc.dram_tensor("d_hT", [128, KC * (128 + T)], BF16, kind="ExternalOutput").ap(),
                "d_hH": nc.dram_tensor("d_hH", [128, KC * T], BF16, kind="ExternalOutput").ap(),
                "d_oT": nc.dram_tensor("d_oT", [128, 4 * T], BF16, kind="ExternalOutput").ap(),
                "d_sT": nc.dram_tensor("d_sT", [128, KC * T], BF16, kind="ExternalOutput").ap(),
                "d_mT": nc.dram_tensor("d_mT", [128, KC * T], BF16, kind="ExternalOutput").ap(),
                "d_x1": nc.dram_tensor("d_x1", [128, 16 * D], F32, kind="ExternalOutput").ap(),
            }

        with ExitStack() as st0:
            self.S = Sched(nc, st0)
            S = self.S
            A = Arena(nc)
            self.A = A
            pst = [st0.enter_context(nc.psum_tensor("ps%d" % i, [128, 1024], F32)) for i in range(4)]
            self.pst = pst
            self.bank = [pst[k // 2][:, (k % 2) * 512:(k % 2 + 1) * 512] for k in range(8)]
            self.bb = [Buf("bank%d" % k, psum=True) for k in range(8)]

            self.ident = A.alloc("ident", [128, 128], F32)
            self.onesm = A.alloc("onesm", [128, 128], BF16)
            self.onesr = A.alloc("onesr", [128, 64], BF16)
            self.nhalf = A.alloc("nhalf", [128, 512], F32)
            self.vecT = A.alloc("vecT", [128, 8, 8], F32)
            self.wdwT = A.alloc("wdwT", [128, 8, CW], F32)
            self.cmask = A.alloc("cmask", [128, 2], F32)
            self.stat = A.alloc("stat", [128, 4, 64], F32)
            self.b_const = Buf("const")
            self.b_vec, self.b_wdw, self.b_cm = Buf(), Buf(), Buf()
            S.op("pool", lambda e: e.memset(self.ident[:], 0.0), pwrites=[self.b_const])
            S.op("pool", lambda e: e.affine_select(out=self.ident[:], in_=self.ident[:], pattern=[[-1, 128]],
                                                   compare_op=ALU.not_equal, fill=1.0, base=0,
                                                   channel_multiplier=1),
                 reads=[self.b_const], pwrites=[self.b_const])
            S.op("dve", lambda e: e.memset(self.onesm[:], 1.0 / 1024.0), pwrites=[self.b_const])
            S.op("dve", lambda e: e.memset(self.onesr[:], 1.0), pwrites=[self.b_const])
            S.op("dve", lambda e: e.memset(self.nhalf[:], -0.5), pwrites=[self.b_const])
            self.load("sp", self.vecT[:, :, :].rearrange("p a b -> p (a b)"), self.vecT_d, self.b_vec)
            self.load("sp", self.wdwT[:, :, :].rearrange("p a b -> p (a b)"), self.wdwT_d, self.b_wdw)
            self.load("sp", self.cmask[:], self.cmask_d, self.b_cm)
            base_lo = A.lo
            if self.stop == 0:
                S.emit()
                return nc

            self.hT_own = A.alloc("hT_own", [128, KC, 128 + T], BF16)
            self.b_hown = Buf("hT_own")
            self.oT = A.alloc("oT", [128, 4, T], BF16)
            self.b_oT = Buf("oT")
            lo_after_hown = A.lo
            self.hT_halo = A.alloc("hT_halo", [128, KC, T], BF16)
            self.b_hhalo = Buf("hT_halo")
            lo_after_halo = A.lo
            S.renew(self.b_hown), S.renew(self.b_hhalo)
            self.wq = [A.alloc("wq%d" % i, [128, 3, 1024], BF16) for i in range(2)]
            self.b_wq = [Buf() for _ in range(2)]
            self.abias = [A.alloc("bias%d" % i, [128, 2, 512], F32) for i in range(3)]
            self.b_abias = [Buf() for _ in range(3)]
            for u in range(2):
                self.load("pool", self.wq[u][:, :, :], self.w_qkv[u], self.b_wq[u])
                self.load("sp", self.abias[u][:, :, :].rearrange("p a b -> p (a b)"), self.bias_d[u], self.b_abias[u])
            lo_after_p2w = A.lo
            self.phase1()
            if self.debug:
                self.store("sp", self.dbg["d_hT"], self.hT_own[:, :, :].rearrange("p a b -> p (a b)"), self.b_hown)
                self.store("sp", self.dbg["d_hH"], self.hT_halo[:, :, :].rearrange("p a b -> p (a b)"), self.b_hhalo)
            if self.stop == 1:
                S.emit()
                return nc
            S.barrier()
            A.lo = lo_after_p2w
            S.renew(self.b_oT)
            self.phase2()
            if self.debug:
                self.store("sp", self.dbg["d_oT"], self.oT[:, :, :].rearrange("p a b -> p (a b)"), self.b_oT)
            if self.stop == 2:
                S.emit()
                return nc
            S.barrier()
            A.lo = lo_after_hown
            self.sT = A.alloc("sT", [128, KC, T], BF16)
            self.b_sT = [[Buf() for _ in range(4)] for _ in range(KC)]
            self.wao = A.alloc("wao", [128, 4, 1024], BF16)
            self.wco = A.alloc("wco", [128, 8, 1024], BF16)
            self.wgt = [A.alloc("wgt%d" % i, [128, 2, 1024], BF16) for i in range(3)]
            self.b_wao, self.b_wco = Buf(), Buf()
            self.b_wgt = [Buf() for _ in range(3)]
            lo_after_sT = A.lo
            self.phase3()
            if self.debug:
                for c in range(KC):
                    for tt in range(4):
                        self.store("sp", self.dbg["d_sT"][:, c * T + tt * 512:c * T + (tt + 1) * 512],
                                   self.sT[:, c, tt * 512:(tt + 1) * 512], self.b_sT[c][tt])
            if self.stop == 3:
                S.emit()
                return nc
            S.barrier()
            A.lo = lo_after_sT
            self.mT = A.alloc("mT", [128, KC, T], BF16, side="R")
            self.b_mT = Buf("mT")
            S.renew(self.b_mT)
            self.wmo = A.alloc("wmo", [128, 8, 1024], BF16, side="R")
            self.b_wmo = Buf()
            self.load("pool", self.wmo[:, :, :], self.w_mo, self.b_wmo)
            self.phase4()
            if self.debug:
                self.store("sp", self.dbg["d_mT"], self.mT[:, :, :].rearrange("p a b -> p (a b)"), self.b_mT)
            if self.stop == 4:
                S.emit()
                return nc
            S.barrier()
            A.lo = base_lo
            self.x1 = A.alloc("x1", [128, 16, D], F32)
            self.b_x1 = [Buf() for _ in range(16)]
            self.wf1 = [A.alloc("wf1_%d" % i, [128, 2, 1024], BF16) for i in range(4)]
            self.b_wf1 = [Buf() for _ in range(4)]
            self.gbc6 = A.alloc("gbc6", [128, 1024], F32)
            self.b_gbc6 = Buf()
            self.h2T = [A.alloc("h2T%d" % i, [128, KC, 512], BF16) for i in range(2)]
            self.b_h2T = [Buf() for _ in range(2)]
            self.scr6 = self.alloc_norm_scratch("p6", nxs=4)
            lo_after_x1 = A.lo
            self.phase5()
            if self.debug:
                for t in range(16):
                    self.store("sp", self.dbg["d_x1"][:, t * D:(t + 1) * D], self.x1[:, t, :], self.b_x1[t])
            if self.stop == 5:
                S.emit()
                return nc
            S.barrier()
            A.lo = lo_after_x1
            A.hi = A.LIMIT
            self.phase6()
            S.emit()
        return nc

    def norm_stages(self, scr, src_fn, dst_fn, gcol, col0, banks=(0, 1, 2, 3)):
        S = self.S
        junk, xs, b_xs = scr
        nxs = len(xs)
        bt = {}
        ss, ms, rs = self.stat[:, 0, :], self.stat[:, 1, :], self.stat[:, 2, :]

        def stats(t):
            ap, buf = src_fn(t)
            cl = slice(col0 + t, col0 + t + 1)
            bt[t] = (Buf(), Buf(), Buf())
            b_ss, b_ms, b_rs = bt[t]
            self.sq_accum(junk, ap, [buf], b_ss, ss[:, cl])
            self.ts("dve", ms[:, cl], ss[:, cl], 1.0 / D, RMS_EPS, ALU.mult, ALU.add, reads=[b_ss], pwrites=[b_ms])
            self.rsqrt_act(rs[:, cl], ms[:, cl], self.stat[:, 3, cl], b_ms, b_rs)

        def scale(t):
            ap, buf = src_fn(t)
            cl = slice(col0 + t, col0 + t + 1)
            i = t % nxs
            self.act(xs[i][:, :], ap, AF.Identity, reads=[buf, bt[t][2]], writes=[b_xs[i]], scale=rs[:, cl])

        def transpose(t):
            i = t % nxs
            pb = (banks[0], banks[1]) if t % 2 == 0 else (banks[2], banks[3])
            for half in range(2):
                def tr(e, half=half, i=i, pb=pb):
                    ins = None
                    for q in range(4):
                        kc = half * 4 + q
                        ins = e.transpose(self.bank[pb[half]][:, q * 128:(q + 1) * 128],
                                          xs[i][:, kc * 128:(kc + 1) * 128], self.ident[:, :])
                    return ins
                S.op("pe", tr, reads=[b_xs[i], self.b_const], writes=[self.bb[pb[half]]])
            for kc in range(KC):
                srcp = self.bank[pb[kc // 4]][:, (kc % 4) * 128:(kc % 4 + 1) * 128]
                for (dst, dbuf) in dst_fn(t, kc):
                    if kc < 4 or self.evac_dve_only:
                        self.ts("dve", dst, srcp, self.vecT[:, kc, gcol:gcol + 1], None, ALU.mult,
                                reads=[self.bb[pb[kc // 4]], self.b_vec], pwrites=[dbuf])
                    else:
                        self.act(dst, srcp, AF.Identity, reads=[self.bb[pb[kc // 4]], self.b_vec], pwrites=[dbuf],
                                 scale=self.vecT[:, kc, gcol:gcol + 1])

        return stats, scale, transpose

    def norm_transpose(self, scr, src_fn, n_tiles, dst_fn, gcol, col0):
        stats, scale, transpose = self.norm_stages(scr, src_fn, dst_fn, gcol, col0)
        stats(0)
        if n_tiles > 1:
            stats(1)
        scale(0)
        for t in range(n_tiles):
            if t + 2 < n_tiles:
                stats(t + 2)
            if t + 1 < n_tiles:
                scale(t + 1)
            transpose(t)

    def rsqrt_act(self, rs_ap, ms_ap, ln_ap, b_ms, b_rs):
        b_ln = Buf()
        self.act(ln_ap, ms_ap, AF.Ln, reads=[b_ms], writes=[b_ln])
        self.act(rs_ap, ln_ap, AF.Exp, reads=[b_ln], pwrites=[b_rs], scale=-0.5)

    def sq_accum(self, junk, ap, reads, b_ss, ss_ap):
        tiles, bufs, k = junk
        i = k[0] % len(tiles)
        k[0] += 1
        self.act(tiles[i][:, :], ap, AF.Square, reads=reads, writes=[bufs[i]], pwrites=[b_ss], accum=ss_ap)

    def alloc_norm_scratch(self, tag, nxs=2):
        A = self.A
        junk = [A.alloc("junk%s%d" % (tag, i), [128, D], BF16) for i in range(2)]
        xs = [A.alloc("xs%s%d" % (tag, i), [128, D], F32) for i in range(nxs)]
        return (junk, [Buf() for _ in range(2)], [0]), xs, [Buf() for _ in range(nxs)]

    def phase1(self):
        A = self.A
        NS = 4
        xin = [A.alloc("xin%d" % i, [128, D], F32) for i in range(NS)]
        b_xin = [Buf() for _ in range(NS)]
        scr = self.alloc_norm_scratch("p1")
        loaded = {}

        def src(t):
            i = t % NS
            if t not in loaded:
                loaded[t] = True
                d = self.x_halo if t < 16 else self.x_own
                r = (t % 16) * 128
                self.load("sp", xin[i][:, :], d[r:r + 128, :], b_xin[i])
            return xin[i][:, :], b_xin[i]

        def dst(t, kc):
            if t < 16:
                out = [(self.hT_halo[:, kc, t * 128:(t + 1) * 128], self.b_hhalo)]
                if t == 15:
                    out.append((self.hT_own[:, kc, 0:128], self.b_hown))
                return out
            return [(self.hT_own[:, kc, 128 + (t - 16) * 128:128 + (t - 15) * 128], self.b_hown)]

        self.evac_dve_only = True
        self.norm_transpose(scr, src, 32, dst, 0, 0)
        self.evac_dve_only = False

    def phase2(self):
        S = self.S
        A = self.A
        hT_own, hT_halo = self.hT_own, self.hT_halo
        NW = 2
        NB = 3
        wq, b_wq, bias, b_bias = self.wq, self.b_wq, self.abias, self.b_abias
        qT = [A.alloc("qT%d" % i, [128, T], BF16) for i in range(2)]
        kT = [A.alloc("kT%d" % i, [128, 2 * T], BF16) for i in range(2)]
        Vt = [A.alloc("V%d" % i, [128, 32, 2, 128], BF16) for i in range(2)]
        b_q = [Buf() for _ in range(2)]
        b_k = [Buf() for _ in range(2)]
        b_v = [Buf() for _ in range(2)]
        NT = 3
        tmp = [A.alloc("tmp%d" % i, [128, 512], F32) for i in range(NT)]
        b_tmp = [Buf() for _ in range(NT)]
        NP = 6
        Pb = [A.alloc("P%d" % i, [128, 512], BF16) for i in range(NP)]
        b_P = [Buf() for _ in range(NP)]
        acc = A.alloc("acc", [128, 2, T], F32)
        b_acc = Buf()
        lnd = [A.alloc("lnd%d" % i, [64, 512], F32) for i in range(2)]
        rb = [A.alloc("rb%d" % i, [64, 512], F32) for i in range(2)]
        b_lnd = [Buf() for _ in range(2)]
        b_rb = [Buf() for _ in range(2)]
        oT, b_oT = self.oT, self.b_oT
        b_ones = [Buf(), Buf()]
        for i in range(2):
            S.op("pool", lambda e, i=i: e.memset(Vt[i][:, :, :, 64:128], 1.0), writes=[b_ones[i]])

        proj_rr = [0]
        cnt_tmp = [0]
        cnt_P = [0]
        cnt_ev = [0]

        def proj_bank():
            k = proj_rr[0] % 2
            proj_rr[0] += 1
            return k

        def params(u):
            hp, g = divmod(u, 3)
            dl = DILS[g]
            M = T // dl
            nb = M // 128
            return hp, g, u % 2, dl, M, nb, 128 + M

        def loads(u):
            if u >= 12:
                return
            self.load("pool", wq[u % NW][:, :, :], self.w_qkv[u], b_wq[u % NW])
            self.load("sp", bias[u % NB][:, :, :].rearrange("p a b -> p (a b)"), self.bias_d[u], b_bias[u % NB])

        def proj_steps(u):
            hp, g, s, dl, M, nb, KW = params(u)
            w = wq[u % NW]
            bw = b_wq[u % NW]
            steps = []
            qv = qT[s][:, :].rearrange("p (c m) -> p c m", c=dl)
            kv = kT[s][:, 0:dl * KW].rearrange("p (c m) -> p c m", c=dl)

            def evac(dst, srcp, pk, dbuf):
                if cnt_ev[0] % 3 == 2:
                    self.copy("dve", dst, srcp, reads=[self.bb[pk]], pwrites=[dbuf])
                else:
                    self.act(dst, srcp, AF.Identity, reads=[self.bb[pk]], pwrites=[dbuf])
                cnt_ev[0] += 1

            def renew_all():
                S.renew(b_q[s]), S.renew(b_k[s]), S.renew(b_v[s])
            steps.append(renew_all)
            for which, dstv, dbuf, moff in ((0, qv, b_q[s], 0), (1, kv, b_k[s], 128)):
                for tt in range(4):
                    def st(which=which, dstv=dstv, dbuf=dbuf, moff=moff, tt=tt):
                        pk = proj_bank()
                        self.mm(self.bank[pk][:, :],
                                [(w[:, which, kc * 128:(kc + 1) * 128],
                                  hT_own[:, kc, 128 + tt * 512:128 + (tt + 1) * 512]) for kc in range(KC)],
                                reads=[bw, self.b_hown], writes=[self.bb[pk]])
                        m0 = tt * 512 // dl
                        srcp = self.bank[pk][:, :].rearrange("p (m c) -> p c m", c=dl)
                        evac(dstv[:, :, moff + m0:moff + m0 + 512 // dl], srcp, pk, dbuf)
                    steps.append(st)
            nh = 128 * dl
            h0 = T - nh
            for t0 in range(0, nh, 512):
                def st(t0=t0):
                    n = min(512, nh - t0)
                    pk = proj_bank()
                    self.mm(self.bank[pk][:, 0:n],
                            [(w[:, 1, kc * 128:(kc + 1) * 128], hT_halo[:, kc, h0 + t0:h0 + t0 + n])
                             for kc in range(KC)],
                            reads=[bw, self.b_hhalo], writes=[self.bb[pk]])
                    m0 = t0 // dl
                    srcp = self.bank[pk][:, 0:n].rearrange("p (m c) -> p c m", c=dl)
                    evac(kv[:, :, m0:m0 + n // dl], srcp, pk, b_k[s])
                steps.append(st)
            nblk = dl * (nb + 1)
            for b0 in range(0, nblk, 4):
                def st(b0=b0):
                    pk = proj_bank()
                    nq4 = min(4, nblk - b0)

                    def vfn(e):
                        ins = None
                        for q in range(nq4):
                            c, j = divmod(b0 + q, nb + 1)
                            for kc in range(KC):
                                if j == 0:
                                    a0 = h0 + c
                                    lhs = hT_halo[:, kc, a0:a0 + 127 * dl + 1:dl]
                                else:
                                    a0 = 128 + dl * 128 * (j - 1) + c
                                    lhs = hT_own[:, kc, a0:a0 + 127 * dl + 1:dl]
                                ins = e.matmul(self.bank[pk][:, q * 128:(q + 1) * 128], lhs,
                                               w[:, 2, kc * 128:(kc + 1) * 128],
                                               start=(kc == 0), stop=(kc == KC - 1))
                        return ins
                    S.op("pe", vfn, reads=[bw, self.b_hown, self.b_hhalo], writes=[self.bb[pk]])
                    srcp = self.bank[pk][:, 0:nq4 * 128].rearrange("p (b d) -> p b d", d=64)
                    dst = Vt[s][:, b0:b0 + nq4, :, 0:64].rearrange("p b e d -> p (b e) d")
                    evac(dst, srcp, pk, b_v[s])
                steps.append(st)
            return steps

        def attention(u, filler, deferred=None):
            hp, g, s, dl, M, nb, KW = params(u)
            bs = bias[u % NB]
            b_bs = b_bias[u % NB]
            items = []
            for c in range(dl):
                for j0 in range(0, nb + 1, 2):
                    items.append((c, [j for j in (j0, j0 + 1) if j <= nb]))

            def cols_of(j, jj):
                lo_ = jj * 256 + (128 if j == 0 else 0)
                hi_ = jj * 256 + (128 if j == nb else 256)
                return lo_, hi_

            def scores(it, sset):
                c, jl = it
                for e_ in range(2):
                    pk = 2 + sset * 2 + e_

                    def sfn(e, e_=e_, pk=pk):
                        ins = None
                        for jj, j in enumerate(jl):
                            lo_, hi_ = cols_of(j, jj)
                            qb0 = j - 1 if j > 0 else 0
                            nq = (hi_ - lo_) // 128
                            ins = e.matmul(self.bank[pk][:, lo_:hi_],
                                           kT[s][64 * e_:64 * e_ + 64, c * KW + j * 128:c * KW + (j + 1) * 128],
                                           qT[s][64 * e_:64 * e_ + 64, c * M + qb0 * 128:c * M + (qb0 + nq) * 128],
                                           start=True, stop=True)
                        return ins
                    S.op("pe", sfn, reads=[b_q[s], b_k[s]], writes=[self.bb[pk]])

            def softmax(it, sset):
                c, jl = it
                lo_ = cols_of(jl[0], 0)[0]
                hi_ = cols_of(jl[-1], len(jl) - 1)[1]
                res = []
                for e_ in range(2):
                    pk = 2 + sset * 2 + e_
                    ti = cnt_tmp[0] % NT
                    cnt_tmp[0] += 1
                    pi = cnt_P[0] % NP
                    cnt_P[0] += 1
                    self.stt(tmp[ti][:, lo_:hi_], self.bank[pk][:, lo_:hi_], 0.125, bs[:, e_, lo_:hi_],
                             ALU.mult, ALU.add, reads=[self.bb[pk], b_bs], writes=[b_tmp[ti]])
                    if jl[0] == 0:
                        self.act(Pb[pi][:, lo_:lo_ + 128], tmp[ti][:, lo_:lo_ + 128], AF.Exp,
                                 reads=[b_tmp[ti], self.b_cm], writes=[b_P[pi]], bias=self.cmask[:, 1:2])
                        if hi_ > lo_ + 128:
                            self.act(Pb[pi][:, lo_ + 128:hi_], tmp[ti][:, lo_ + 128:hi_], AF.Exp,
                                     reads=[b_tmp[ti]], pwrites=[b_P[pi]])
                    else:
                        self.act(Pb[pi][:, lo_:hi_], tmp[ti][:, lo_:hi_], AF.Exp,
                                 reads=[b_tmp[ti]], writes=[b_P[pi]])
                    res.append(pi)
                return res

            def evac_pv(first_lin, nblocks):
                for e_ in range(2):
                    pk = 6 + e_
                    if nb >= 4:
                        c, b = divmod(first_lin, nb)
                        a0 = dl * 128 * b + c
                        n = 128 * nblocks
                        dst = acc[:, e_, a0:a0 + (n - 1) * dl + 1:dl]
                        srcp = self.bank[pk][:, 0:n]
                    else:
                        c0 = first_lin
                        dst = acc[:, e_, :].rearrange("p (a c) -> p c a", c=dl)[:, c0:c0 + nblocks, :]
                        srcp = self.bank[pk][:, 0:128 * nblocks].rearrange("p (c a) -> p c a", a=128)
                    if g == 0:
                        self.copy("dve", dst, srcp, reads=[self.bb[pk]], pwrites=[b_acc])
                    else:
                        self.tt("dve", dst, dst, srcp, ALU.add, reads=[self.bb[pk], b_acc], pwrites=[b_acc])

            def pv(it, pis):
                acts = []
                c, jl = it
                for jj, j in enumerate(jl):
                    blk = c * (nb + 1) + j
                    roles = []
                    if j > 0:
                        roles.append(("cur", j - 1, jj * 256))
                    if j < nb:
                        roles.append(("prev", j, jj * 256 + 128))
                    for role, b, pcol in roles:
                        lin = c * nb + b
                        slot = lin % 4
                        for e_ in range(2):
                            pk = 6 + e_
                            first = (role == "prev")
                            pi = pis[e_]

                            def pfn(e, pk=pk, slot=slot, blk=blk, e_=e_, pi=pi, pcol=pcol, first=first):
                                return e.matmul(self.bank[pk][:, slot * 128:(slot + 1) * 128],
                                                Vt[s][:, blk, e_, :], Pb[pi][:, pcol:pcol + 128],
                                                start=first, stop=(not first))
                            if first and slot == 0:
                                acts.append((False, lambda pfn=pfn, pi=pi, pk=pk: S.op(
                                    "pe", pfn, reads=[b_v[s], b_ones[s], b_P[pi]], writes=[self.bb[pk]])))
                            else:
                                acts.append((False, lambda pfn=pfn, pi=pi, pk=pk: S.op(
                                    "pe", pfn, reads=[b_v[s], b_ones[s], b_P[pi]], pwrites=[self.bb[pk]])))
                        if role == "cur" and (slot == 3 or lin == dl * nb - 1):
                            acts.append((True, lambda lin=lin, slot=slot: evac_pv(lin - slot, slot + 1)))
                k = 0
                while k < len(acts):
                    is_evac, fn = acts[k]
                    fn()
                    k += 1
                    if is_evac:
                        break
                rest = acts[k:]
                if not rest:
                    return None

                def run_rest():
                    for _, fn in rest:
                        fn()
                return run_rest

            nfill = len(filler)
            nit = len(items)
            fi = 0
            prev = None
            pend = None
            for idx, it in enumerate(items):
                sset = idx % 2
                scores(it, sset)
                pis = softmax(it, sset)
                tgt = (idx + 1) * nfill // nit
                while fi < tgt:
                    filler[fi]()
                    fi += 1
                if pend is not None:
                    pend()
                    pend = None
                if prev is not None:
                    pend = pv(*prev)
                prev = (it, pis)
                if idx == 0:
                    if deferred is not None:
                        deferred()
                    if g == 0:
                        S.renew(b_acc)
            if pend is not None:
                pend()
            pend = pv(*prev)
            if pend is not None:
                pend()
            while fi < nfill:
                filler[fi]()
                fi += 1

        def normalize(hp):
            k = 0
            for e_ in range(2):
                for tt in range(4):
                    cs = slice(tt * 512, (tt + 1) * 512)
                    i = k % 2
                    k += 1
                    self.act(lnd[i][:, :], acc[64:128, e_, cs], AF.Ln, reads=[b_acc], writes=[b_lnd[i]])
                    self.act(rb[i][:, :], lnd[i][:, :], AF.Exp, reads=[b_lnd[i]], writes=[b_rb[i]], scale=-1.0)
                    self.tt("dve", oT[64 * e_:64 * e_ + 64, hp, cs], acc[0:64, e_, cs], rb[i][:, :], ALU.mult,
                            reads=[b_acc, b_rb[i]], pwrites=[b_oT])

        for st in proj_steps(0):
            st()
        pending = None
        for u in range(12):
            loads(u + 2)
            filler = proj_steps(u + 1) if u + 1 < 12 else []
            attention(u, filler, pending)
            pending = None
            if u % 3 == 2:
                if u == 11:
                    normalize(u // 3)
                else:
                    pending = (lambda hp=u // 3: normalize(hp))

    def phase3(self):
        S = self.S
        A = self.A
        hT_own = self.hT_own
        sT, b_sT = self.sT, self.b_sT
        uT = A.alloc("uT", [128, KC, HW + T], BF16)
        b_uT = [Buf() for _ in range(KC)]
        lo_wg = A.lo
        wg = [A.alloc("wglu%d" % i, [128, 2, 1024], BF16) for i in range(2)]
        b_wg = [Buf() for _ in range(2)]
        sg = [A.alloc("sg%d" % i, [128, 512], F32) for i in range(2)]
        b_sg = [Buf() for _ in range(2)]
        diag = [A.alloc("diag%d" % i, [128, CW, 128], BF16) for i in range(2)]
        b_dg = [Buf() for _ in range(2)]
        sq = [A.alloc("sq%d" % i, [128, 512], BF16) for i in range(2)]
        b_sq = [Buf() for _ in range(2)]
        m2 = A.alloc("m2", [128, 512], F32)
        veps = A.alloc("veps", [128, 512], F32)
        rstd = A.alloc("rstd", [128, 512], F32)
        b_m2, b_ve, b_rstd = Buf(), Buf(), Buf()
        t1 = [A.alloc("t1_%d" % i, [128, 512], F32) for i in range(2)]
        t2 = [A.alloc("t2_%d" % i, [128, 512], F32) for i in range(2)]
        b_t1 = [Buf() for _ in range(2)]
        b_t2 = [Buf() for _ in range(2)]
        k_sg = 0
        pr = 0
        for c in range(KC):
            s = c % 2
            self.load("pool", wg[s][:, :, :], self.w_glu[c], b_wg[s])
            S.renew(b_uT[c])
            for tt in range(-1, 4):
                if tt < 0:
                    cols = slice(128 - HW, 128)
                    n = HW
                    dst = uT[:, c, 0:HW]
                else:
                    cols = slice(128 + tt * 512, 128 + (tt + 1) * 512)
                    n = 512
                    dst = uT[:, c, HW + tt * 512:HW + (tt + 1) * 512]
                pa, pb_ = pr % 4 * 2, pr % 4 * 2 + 1
                pr += 1
                for which, pk in ((0, pa), (1, pb_)):
                    self.mm(self.bank[pk][:, 0:n],
                            [(wg[s][:, which, kc * 128:(kc + 1) * 128], hT_own[:, kc, cols]) for kc in range(KC)],
                            reads=[b_wg[s], self.b_hown], writes=[self.bb[pk]])
                i = k_sg % 2
                k_sg += 1
                self.act(sg[i][:, 0:n], self.bank[pb_][:, 0:n], AF.Sigmoid, reads=[self.bb[pb_], self.b_vec],
                         writes=[b_sg[i]], bias=self.vecT[:, c, 2:3])
                self.stt(dst, self.bank[pa][:, 0:n], self.vecT[:, c, 1:2], sg[i][:, 0:n], ALU.add, ALU.mult,
                         reads=[self.bb[pa], b_sg[i], self.b_vec], pwrites=[b_uT[c]])
                if tt < 0:
                    self.ts("dve", dst, dst, self.cmask[:, 0:1], None, ALU.mult,
                            reads=[b_uT[c], self.b_cm], pwrites=[b_uT[c]])
        self.load("pool", self.wgt[0][:, :, :], self.w_gate[0], self.b_wgt[0])
        self.load("pool", self.wgt[1][:, :, :], self.w_gate[1], self.b_wgt[1])
        self.load("pool", self.wao[:, :, :], self.w_ao, self.b_wao)
        self.load("pool", self.wco[:, :, :], self.w_co, self.b_wco)
        pr = 0
        for c in range(KC):
            s = c % 2
            S.renew(b_dg[s])
            for j in range(CW):
                if j % 2 == 0:
                    self.ts("dve", diag[s][:, j, :], self.ident[:, :], self.wdwT[:, c, j:j + 1], None, ALU.mult,
                            reads=[self.b_const, self.b_wdw], pwrites=[b_dg[s]])
                else:
                    self.act(diag[s][:, j, :], self.ident[:, :], AF.Identity, reads=[self.b_const, self.b_wdw],
                             pwrites=[b_dg[s]], scale=self.wdwT[:, c, j:j + 1])
            for tt in range(4):
                pk = pr % 4
                pr += 1
                base = tt * 512 + HW - (CW - 1)
                self.mm(self.bank[pk][:, :],
                        [(diag[s][:, j, :], uT[:, c, base + j:base + j + 512]) for j in range(CW)],
                        reads=[b_dg[s], b_uT[c]], writes=[self.bb[pk]])
                self.act(sT[:, c, tt * 512:(tt + 1) * 512], self.bank[pk][:, :], AF.Identity,
                         reads=[self.bb[pk], self.b_vec], writes=[b_sT[c][tt]], bias=self.vecT[:, c, 3:4])

    def phase4(self):
        S = self.S
        A = self.A
        hT_own, oT, b_oT, sT, b_sT, mT = self.hT_own, self.oT, self.b_oT, self.sT, self.b_sT, self.mT
        wao, wco, b_wao, b_wco, wgt, b_wgt = self.wao, self.wco, self.b_wao, self.b_wco, self.wgt, self.b_wgt
        sga = [A.alloc("sga%d" % i, [128, 512], F32) for i in range(2)]
        sgc = [A.alloc("sgc%d" % i, [128, 512], F32) for i in range(2)]
        b_sga = [Buf() for _ in range(2)]
        b_sgc = [Buf() for _ in range(2)]
        sq = [A.alloc("sq%d" % i, [128, 512], BF16) for i in range(2)]
        b_sq = [Buf() for _ in range(2)]
        m2b = [A.alloc("m2_0", [128, 512], F32)] * 2
        vepsb = [A.alloc("veps0", [128, 512], F32)] * 2
        lnv = [A.alloc("lnv0", [128, 512], F32)] * 2
        rstdb = [A.alloc("rstd%d" % i, [128, 512], F32) for i in range(2)]
        nmrb = [A.alloc("nmr%d" % i, [128, 512], F32) for i in range(2)]
        b_m2s = [Buf()] * 2
        b_ves = [Buf()] * 2
        b_lnv = [Buf()] * 2
        b_rstds = [Buf() for _ in range(2)]
        b_nmrs = [Buf() for _ in range(2)]
        t1 = [A.alloc("t1_%d" % i, [128, 512], F32) for i in range(2)]
        b_t1 = [Buf() for _ in range(2)]
        cnt = {"sq": 0, "t": 0}

        def ln_a(tt):
            cs = slice(tt * 512, (tt + 1) * 512)
            pm, pe2 = 0, 1
            S.renew(self.bb[pm]), S.renew(self.bb[pe2])
            for c in range(KC):
                i = cnt["sq"] % 2
                cnt["sq"] += 1
                self.act(sq[i][:, :], sT[:, c, cs], AF.Square, reads=[b_sT[c][tt]], writes=[b_sq[i]])
                self.mm(self.bank[pm][:, :], [(self.onesm[:, :], sT[:, c, cs])],
                        reads=[self.b_const, b_sT[c][tt]], pwrites=[self.bb[pm]],
                        start=(c == 0), stop=(c == KC - 1))
                self.mm(self.bank[pe2][:, :], [(self.onesm[:, :], sq[i][:, :])],
                        reads=[self.b_const, b_sq[i]], pwrites=[self.bb[pe2]],
                        start=(c == 0), stop=(c == KC - 1))

        def ln_b(tt):
            k = tt % 2
            pm, pe2 = 0, 1
            self.act(m2b[k][:, :], self.bank[pm][:, :], AF.Square, reads=[self.bb[pm]], writes=[b_m2s[k]])
            self.stt(vepsb[k][:, :], self.bank[pe2][:, :], LN_EPS, m2b[k][:, :], ALU.add, ALU.subtract,
                     reads=[self.bb[pe2], b_m2s[k]], writes=[b_ves[k]])
            self.act(lnv[k][:, :], vepsb[k][:, :], AF.Ln, reads=[b_ves[k]], writes=[b_lnv[k]])
            self.act(rstdb[k][:, :], lnv[k][:, :], AF.Exp, reads=[b_lnv[k]], writes=[b_rstds[k]], scale=-0.5)
            self.stt(nmrb[k][:, :], self.bank[pm][:, :], -1.0, rstdb[k][:, :], ALU.mult, ALU.mult,
                     reads=[self.bb[pm], b_rstds[k]], writes=[b_nmrs[k]])

        def ln_c(tt, c):
            k = tt % 2
            cs = slice(tt * 512, (tt + 1) * 512)
            i = cnt["t"] % 2
            cnt["t"] += 1
            self.tt("dve", t1[i][:, :], sT[:, c, cs], rstdb[k][:, :], ALU.mult,
                    reads=[b_sT[c][tt], b_rstds[k]], writes=[b_t1[i]])
            self.tt("dve", t1[i][:, :], t1[i][:, :], nmrb[k][:, :], ALU.add,
                    reads=[b_nmrs[k]], writes=[b_t1[i]])
            self.act(sT[:, c, cs], t1[i][:, :], AF.Silu, reads=[b_t1[i], self.b_vec], writes=[b_sT[c][tt]],
                     scale=self.vecT[:, c, 4:5], bias=self.vecT[:, c, 5:6])

        NWG = len(wgt)
        steps = [(tt, c) for tt in range(4) for c in range(KC)]

        def wload(n):
            if 2 <= n < len(steps):
                c = steps[n][1]
                self.load("pool", wgt[n % NWG][:, :, :], self.w_gate[c], b_wgt[n % NWG])

        ln_a(0)
        ln_b(0)
        ln_a(1)
        ln_b(1)
        for c in range(KC):
            ln_c(0, c)
        wload(2)
        for n, (tt, c) in enumerate(steps):
            wload(n + 2) if n >= 1 else None
            s = n % NWG
            cs = slice(tt * 512, (tt + 1) * 512)
            hs = slice(128 + tt * 512, 128 + (tt + 1) * 512)
            i = n % 2
            p0 = (n % 2) * 4
            pga, pgc, pya, pyc = p0, p0 + 1, p0 + 2, p0 + 3
            self.mm(self.bank[pga][:, :],
                    [(wgt[s][:, 0, kc * 128:(kc + 1) * 128], hT_own[:, kc, hs]) for kc in range(KC)],
                    reads=[b_wgt[s], self.b_hown], writes=[self.bb[pga]])
            self.mm(self.bank[pgc][:, :],
                    [(wgt[s][:, 1, kc * 128:(kc + 1) * 128], hT_own[:, kc, hs]) for kc in range(KC)],
                    reads=[b_wgt[s], self.b_hown], writes=[self.bb[pgc]])
            self.mm(self.bank[pya][:, :],
                    [(wao[:, kc, c * 128:(c + 1) * 128], oT[:, kc, cs]) for kc in range(4)],
                    reads=[b_wao, b_oT], writes=[self.bb[pya]])
            self.mm(self.bank[pyc][:, :],
                    [(wco[:, kc, c * 128:(c + 1) * 128], sT[:, kc, cs]) for kc in range(KC)],
                    reads=[b_wco] + [b_sT[kc][tt] for kc in range(KC)], writes=[self.bb[pyc]])
            self.act(sga[i][:, :], self.bank[pga][:, :], AF.Sigmoid, reads=[self.bb[pga]], writes=[b_sga[i]])
            self.act(sgc[i][:, :], self.bank[pgc][:, :], AF.Sigmoid, reads=[self.bb[pgc]], writes=[b_sgc[i]])
            self.tt("dve", sga[i][:, :], self.bank[pya][:, :], sga[i][:, :], ALU.mult,
                    reads=[self.bb[pya]], writes=[b_sga[i]])
            self.stt(sgc[i][:, :], self.bank[pyc][:, :], self.vecT[:, c, 6:7], sgc[i][:, :], ALU.add, ALU.mult,
                     reads=[self.bb[pyc], self.b_vec], writes=[b_sgc[i]])
            self.tt("dve", mT[:, c, cs], sga[i][:, :], sgc[i][:, :], ALU.add,
                    reads=[b_sga[i], b_sgc[i]], pwrites=[self.b_mT])
            if tt + 1 < 4 and c in (1, 5):
                for cc in range(c - 1, c + 3):
                    ln_c(tt + 1, cc)
            if c == KC - 1 and tt + 2 < 4:
                ln_a(tt + 2)
                ln_b(tt + 2)

    def phase5(self):
        S = self.S
        A = self.A
        x1, b_x1, mT = self.x1, self.b_x1, self.mT
        wmo, b_wmo = self.wmo, self.b_wmo
        for j in range(4):
            self.load("pool", self.wf1[j][:, :, :], self.w_f1[j], self.b_wf1[j])
        self.load("sp", self.gbc6[:, :], self.gbc_d[:, 1, :], self.b_gbc6)
        gbc = A.alloc("gbc5", [128, 1024], F32)
        b_gbc = Buf()
        self.load("sp", gbc[:, :], self.gbc_d[:, 0, :], b_gbc)
        xin = [A.alloc("x5in%d" % i, [128, D], F32) for i in range(2)]
        b_xin = [Buf() for _ in range(2)]
        t1 = [A.alloc("t5_%d" % i, [128, D], F32) for i in range(2)]
        b_t1 = [Buf() for _ in range(2)]
        junk = ([A.alloc("junk5_%d" % i, [128, D], BF16) for i in range(2)], [Buf() for _ in range(2)], [0])
        b_ss, b_ms, b_rs = Buf(), Buf(), Buf()
        ss, ms, rs = self.stat[:, 0, :], self.stat[:, 1, :], self.stat[:, 2, :]
        bt5 = {}

        def front(t):
            i = t % 2
            pp = t % 4
            pa, pb_ = 2 * pp, 2 * pp + 1
            ts_ = slice(t * 128, (t + 1) * 128)
            cl = slice(32 + t, 33 + t)
            self.load("sp", xin[i][:, :], self.x_own[ts_, :], b_xin[i])
            for half, pk in ((0, pa), (1, pb_)):
                self.mm(self.bank[pk][:, :],
                        [(mT[:, kc, ts_], wmo[:, kc, half * 512:(half + 1) * 512]) for kc in range(KC)],
                        reads=[self.b_mT, b_wmo], writes=[self.bb[pk]])
            yps = self.pst[pp][:, :]
            bt5[t] = (Buf(), Buf(), Buf())
            b_ss, b_ms, b_rs = bt5[t]
            self.sq_accum(junk, yps, [self.bb[pa], self.bb[pb_]], b_ss, ss[:, cl])
            self.ts("dve", ms[:, cl], ss[:, cl], 1.0 / D, RMS_EPS, ALU.mult, ALU.add, reads=[b_ss], pwrites=[b_ms])
            self.rsqrt_act(rs[:, cl], ms[:, cl], self.stat[:, 3, cl], b_ms, b_rs)

        def back(t):
            b_rs = bt5[t][2]
            i = t % 2
            pp = t % 4
            pa, pb_ = 2 * pp, 2 * pp + 1
            cl = slice(32 + t, 33 + t)
            yps = self.pst[pp][:, :]
            self.stt(t1[i][:, :], yps, rs[:, cl], gbc[:, :], ALU.mult, ALU.mult,
                     reads=[self.bb[pa], self.bb[pb_], b_rs, b_gbc], writes=[b_t1[i]])
            self.tt("dve", x1[:, t, :], t1[i][:, :], xin[i][:, :], ALU.add,
                    reads=[b_t1[i], b_xin[i]], writes=[b_x1[t]])

        front(0)
        for t in range(16):
            if t + 1 < 16:
                front(t + 1)
            back(t)
        S.renew(self.b_h2T[0])
        self.norm_transpose(self.scr6, lambda i: (x1[:, i, :], b_x1[i]), 4,
                            lambda i, kc: [(self.h2T[0][:, kc, i * 128:(i + 1) * 128], self.b_h2T[0])], 7, 0)

    def phase6(self):
        S = self.S
        A = self.A
        x1, b_x1 = self.x1, self.b_x1
        wf1, b_wf1, gbc, b_gbc, h2T, b_h2T, scr = (self.wf1, self.b_wf1, self.gbc6, self.b_gbc6, self.h2T,
                                                      self.b_h2T, self.scr6)
        wf2 = A.alloc("wf2", [128, NJ, 1024], BF16)
        b_wf2 = Buf()
        aT = A.alloc("aT", [128, NJ, 512], BF16)
        b_aT = [Buf() for _ in range(NJ)]
        sgf = [A.alloc("sgf%d" % i, [128, 512], F32) for i in range(2)]
        b_sgf = [Buf() for _ in range(2)]
        t1 = [A.alloc("t6_%d" % i, [128, D], F32) for i in range(2)]
        b_t1 = [Buf() for _ in range(2)]
        junk = scr[0]
        b_ss, b_ms, b_rs = Buf(), Buf(), Buf()
        ss, ms, rs = self.stat[:, 0, :], self.stat[:, 1, :], self.stat[:, 2, :]
        NWF = len(wf1)
        wf2_pieces = [(0, 6), (6, 12), (12, 17), (17, 22)]
        kj = 0
        for qt in range(4):
            hq = h2T[qt % 2]
            b_hq = b_h2T[qt % 2]
            for j in range(NJ):
                s = kj % NWF
                if kj >= 4:
                    self.load("pool", wf1[s][:, :, :], self.w_f1[j], b_wf1[s])
                if qt == 0 and j < 4:
                    a, b = wf2_pieces[j]
                    self.load("pool", wf2[:, a:b, :], self.w_f2[:, a:b, :], b_wf2, partial=True)
                kj += 1
                pg, pu = 4 + 2 * (j % 2), 5 + 2 * (j % 2)
                for which, pk in ((0, pg), (1, pu)):
                    self.mm(self.bank[pk][:, :],
                            [(wf1[s][:, which, kc * 128:(kc + 1) * 128], hq[:, kc, :]) for kc in range(KC)],
                            reads=[b_wf1[s], b_hq], writes=[self.bb[pk]])
                i = j % 2
                self.act(sgf[i][:, :], self.bank[pg][:, :], AF.Silu, reads=[self.bb[pg]], writes=[b_sgf[i]])
                self.tt("dve", aT[:, j, :], sgf[i][:, :], self.bank[pu][:, :], ALU.mult,
                        reads=[b_sgf[i], self.bb[pu]], writes=[b_aT[j]])
            nxt = None
            if qt + 1 < 4:
                hn = h2T[(qt + 1) % 2]
                b_hn = b_h2T[(qt + 1) % 2]
                S.renew(b_hn)
                nxt = self.norm_stages(
                    scr, lambda i, qt=qt: (x1[:, (qt + 1) * 4 + i, :], b_x1[(qt + 1) * 4 + i]),
                    lambda i, kc, hn=hn, b_hn=b_hn: [(hn[:, kc, i * 128:(i + 1) * 128], b_hn)], 7, (qt + 1) * 4,
                    banks=(4, 5, 6, 7))
                for i4 in range(4):
                    nxt[0](i4)
                for i4 in range(4):
                    nxt[1](i4)
            for i4 in range(4):
                t = qt * 4 + i4
                i = t % 2
                pp = t % 2
                pa, pb_ = 2 * pp, 2 * pp + 1
                cl = slice(48 + t, 49 + t)
                for half, pk in ((0, pa), (1, pb_)):
                    self.mm(self.bank[pk][:, :],
                            [(aT[:, j, i4 * 128:(i4 + 1) * 128], wf2[:, j, half * 512:(half + 1) * 512])
                             for j in range(NJ)],
                            reads=b_aT + [b_wf2], writes=[self.bb[pk]])
                if nxt is not None:
                    nxt[2](i4)
                yps = self.pst[pp][:, :]
                b_ss, b_ms, b_rs = Buf(), Buf(), Buf()
                self.sq_accum(junk, yps, [self.bb[pa], self.bb[pb_]], b_ss, ss[:, cl])
                self.ts("dve", ms[:, cl], ss[:, cl], 1.0 / D, RMS_EPS, ALU.mult, ALU.add, reads=[b_ss], pwrites=[b_ms])
                self.rsqrt_act(rs[:, cl], ms[:, cl], self.stat[:, 3, cl], b_ms, b_rs)
                self.stt(t1[i][:, :], yps, rs[:, cl], gbc[:, :], ALU.mult, ALU.mult,
                         reads=[self.bb[pa], self.bb[pb_], b_rs, b_gbc], writes=[b_t1[i]])
                self.tt("dve", x1[:, t, :], t1[i][:, :], x1[:, t, :], ALU.add,
                        reads=[b_t1[i]], writes=[b_x1[t]])
                self.store("sp", self.y[t * 128:(t + 1) * 128, :], x1[:, t, :], b_x1[t])


def _rel_bucket_np(d):
    d = np.maximum(d, 0)
    df = np.maximum(d, 1).astype(np.float32)
    large = 16 + (np.log(df / np.float32(16)) / np.float32(math.log(2048 / 16)) * np.float32(16)).astype(np.int32)
    large = np.minimum(large, 31)
    return np.where(d < 16, d, large)


def _blk(w, cols):
    n = len(cols)
    return np.ascontiguousarray(w[:, cols].reshape(KC, 128, n).transpose(1, 0, 2)).reshape(128, KC * n)


def _prep_shared(rel_bias_table, g_pre_mix, w_in, b_glu, w_dw, b_dw, g_conv_ln, b_conv_ln, w_conv_out,
                 b_conv_out, w_attn_out, w_mix_out, g_post_mix, g_pre_ffn, w_ffn_in, w_ffn_out, g_post_ffn):
    f = lambda a: np.asarray(a, dtype=np.float32)
    tab = f(rel_bias_table)
    w_in = f(w_in)[0]
    ar = np.arange(128)
    w_qkv = np.empty((12, 128, 3, 1024), np.float32)
    bias = np.empty((12, 128, 2, 4, 128), np.float32)
    kk = ar[:, None]
    qq = ar[None, :]
    for hp in range(4):
        for g in range(3):
            u = hp * 3 + g
            dl = DILS[g]
            for s in range(3):
                w_qkv[u, :, s, :] = _blk(w_in, s * 1536 + g * 512 + hp * 128 + ar)
            idx_cur = _rel_bucket_np((qq - kk) * dl)
            idx_prev = _rel_bucket_np((qq - kk + 128) * dl)
            for e in range(2):
                col = tab[:, g * 8 + hp * 2 + e]
                cur = np.where(qq >= kk, col[idx_cur], np.float32(NEG)).astype(np.float32)
                prev = np.where(kk >= qq, col[idx_prev], np.float32(NEG)).astype(np.float32)
                bias[u, :, e, 0] = cur
                bias[u, :, e, 1] = prev
                bias[u, :, e, 2] = cur
                bias[u, :, e, 3] = prev
    w_glu = np.empty((8, 128, 2, 1024), np.float32)
    w_gate = np.empty((8, 128, 2, 1024), np.float32)
    for c in range(8):
        w_glu[c, :, 0, :] = _blk(w_in, 4608 + c * 128 + ar)
        w_glu[c, :, 1, :] = _blk(w_in, 4608 + 1024 + c * 128 + ar)
        w_gate[c, :, 0, :] = _blk(w_in, 6656 + c * 128 + ar)
        w_gate[c, :, 1, :] = _blk(w_in, 7680 + c * 128 + ar)
    w_f1s = f(w_ffn_in)[0]
    w_f1 = np.empty((NJ, 128, 2, 1024), np.float32)
    for j in range(NJ):
        w_f1[j, :, 0, :] = _blk(w_f1s, j * 128 + ar)
        w_f1[j, :, 1, :] = _blk(w_f1s, FFN_H + j * 128 + ar)
    nat = lambda w, nk: np.ascontiguousarray(f(w)[0].reshape(nk, 128, -1).transpose(1, 0, 2))
    vecs = [f(g_pre_mix)[0], f(b_glu)[0][:1024], f(b_glu)[0][1024:], f(b_dw)[0], f(g_conv_ln)[0], f(b_conv_ln)[0],
            f(b_conv_out)[0], f(g_pre_ffn)[0]]
    vecT = np.ascontiguousarray(np.stack(vecs, axis=0).reshape(8, 8, 128).transpose(2, 1, 0)).reshape(128, 64)
    wdwT = np.ascontiguousarray(f(w_dw)[0].reshape(CW, 8, 128).transpose(2, 1, 0)).reshape(128, 8 * CW)
    gbc = np.ascontiguousarray(np.broadcast_to(np.stack([f(g_post_mix)[0], f(g_post_ffn)[0]])[None], (128, 2, 1024)))
    return {
        "w_qkv": w_qkv, "w_glu": w_glu, "w_gate": w_gate,
        "w_ao": nat(w_attn_out, 4), "w_co": nat(w_conv_out, 8), "w_mo": nat(w_mix_out, 8),
        "w_f1": w_f1, "w_f2": nat(w_ffn_out, NJ),
        "vecT": vecT, "wdwT": wdwT, "gbc": gbc, "bias": bias.reshape(12, 128, 1024),
    }


_NC_CACHE = {}


def _get_nc(debug):
    if debug not in _NC_CACHE:
        _NC_CACHE[debug] = K(debug=debug).build()
    return _NC_CACHE[debug]


def make_in_maps(x, shared, cores):
    x2 = np.asarray(x, dtype=np.float32).reshape(SEQ, D)
    in_maps = []
    for i in cores:
        m = dict(shared)
        m["x_own"] = np.ascontiguousarray(x2[i * T:(i + 1) * T])
        if i == 0:
            m["x_halo"] = np.zeros((T, D), np.float32)
            cm = np.array([0.0, NEG], np.float32)
        else:
            m["x_halo"] = np.ascontiguousarray(x2[(i - 1) * T:i * T])
            cm = np.array([1.0, 0.0], np.float32)
        m["cmask"] = np.ascontiguousarray(np.broadcast_to(cm[None, :], (128, 2)))
        in_maps.append(m)
    return in_maps


def kernel(x, rel_bias_table, g_pre_mix, w_in, b_glu, w_dw, b_dw, g_conv_ln, b_conv_ln, w_conv_out, b_conv_out,
           w_attn_out, w_mix_out, g_post_mix, g_pre_ffn, w_ffn_in, w_ffn_out, g_post_ffn):
    shared = _prep_shared(rel_bias_table, g_pre_mix, w_in, b_glu, w_dw, b_dw, g_conv_ln, b_conv_ln, w_conv_out,
                          b_conv_out, w_attn_out, w_mix_out, g_post_mix, g_pre_ffn, w_ffn_in, w_ffn_out, g_post_ffn)
    nc = _get_nc(False)
    in_maps = make_in_maps(x, shared, list(range(NCORES)))
    res = run_bass_kernel_spmd(nc, in_maps, core_ids=list(range(NCORES)))
    out = np.concatenate([np.asarray(r["y"], dtype=np.float32) for r in res.results], axis=0)
    return out.reshape(1, SEQ, D)
```

```python
import math
import os
from contextlib import ExitStack

import numpy as np
import concourse.bass as bass
import concourse.mybir as mybir
from concourse.bass_utils import run_bass_kernel_spmd

F32 = mybir.dt.float32
BF16 = mybir.dt.bfloat16
AF = mybir.ActivationFunctionType
ALU = mybir.AluOpType
ENGS = ("pe", "act", "dve", "pool", "sp")

NCORES = 8
SEQ = 16384
D = 1024
T = SEQ // NCORES
KC = D // 128
FFN_H = 2816
NJ = FFN_H // 128
DILS = (1, 4, 16)
NEG = -30000.0
RMS_EPS = 1e-6
LN_EPS = 1e-5
HW = 32
CW = 31


class Buf:
    __slots__ = ("w", "r", "pw", "pr", "sem", "semval", "name", "psum")

    def __init__(self, name="", psum=False):
        self.psum = psum
        self.w = {}
        self.r = {}
        self.pw = {}
        self.pr = {}
        self.sem = None
        self.semval = 0
        self.name = name


def _merge(dst, src):
    for k, v in src.items():
        if k not in dst or dst[k][1] < v[1]:
            dst[k] = v


class Sched:
    def __init__(self, nc, stack):
        self.nc = nc
        self.stack = stack
        self.sem = {e: stack.enter_context(nc.semaphore("s_" + e)) for e in ENGS}
        self.cnt = {e: 0 for e in ENGS}
        self.seen = {e: {} for e in ENGS}
        self.prog = {e: [] for e in ENGS}
        self.dma_bufs = []
        self.nsem = 0

    def renew(self, b):
        b.pw = dict(b.w)
        b.pr = dict(b.r)
        b.w = {}
        b.r = {}

    def _waits(self, eng, reads, writes, pwrites):
        waits = {}
        seen = self.seen[eng]

        def need(deps, is_reader):
            for key, (sem, val) in deps.items():
                if key[0] == "e" and key[1] == eng and (eng in ("pe", "sp") or is_reader):
                    continue
                if seen.get(key, 0) >= val:
                    continue
                if key not in waits or waits[key][1] < val:
                    waits[key] = (sem, val)

        for b in reads:
            need(b.w, False)
            if b.psum:
                need(b.r, True)
        for b in writes:
            self.renew(b)
        for b in list(writes) + list(pwrites):
            need(b.pw, False)
            need(b.pr, True)
        for b in pwrites:
            if b.psum:
                need(b.r, True)
        for key, (sem, val) in waits.items():
            seen[key] = val
        return list(waits.values())

    def _record(self, key, me, reads, writes, pwrites):
        for b in reads:
            b.r[key] = me
        for b in list(writes) + list(pwrites):
            b.w[key] = me

    def op(self, eng, fn, reads=(), writes=(), pwrites=()):
        waits = self._waits(eng, reads, writes, pwrites)
        self.cnt[eng] += 1
        self.prog[eng].append((waits, fn, (self.sem[eng], 1)))
        self._record(("e", eng), (self.sem[eng], self.cnt[eng]), reads, writes, pwrites)

    def dma(self, eng, fn, sbuf, is_load, reads=(), writes=(), pwrites=(), partial=False):
        reads = list(reads)
        writes = list(writes)
        pwrites = list(pwrites)
        if is_load:
            (pwrites if partial else writes).append(sbuf)
        else:
            reads.append(sbuf)
        waits = self._waits(eng, reads, writes, pwrites)
        if sbuf.sem is None:
            self.nsem += 1
            sbuf.sem = self.stack.enter_context(self.nc.semaphore("d%d" % self.nsem))
            self.dma_bufs.append(sbuf)
        sbuf.semval += 16
        self.prog[eng].append((waits, fn, (sbuf.sem, 16)))
        self._record(("d", id(sbuf)), (sbuf.sem, sbuf.semval), reads, writes, pwrites)

    def barrier(self):
        for e in ENGS:
            waits = []
            for e2 in ENGS:
                if (e2 != e or e in ("act", "dve", "pool")) and self.cnt[e2] > self.seen[e].get(("e", e2), 0):
                    waits.append((self.sem[e2], self.cnt[e2]))
                    self.seen[e][("e", e2)] = self.cnt[e2]
            for b in self.dma_bufs:
                key = ("d", id(b))
                if b.semval > self.seen[e].get(key, 0):
                    waits.append((b.sem, b.semval))
                    self.seen[e][key] = b.semval
            if waits:
                self.prog[e].append((waits, None, None))

    def emit(self):
        nc = self.nc
        self.barrier()
        prog = self.prog

        def run(e, name):
            for waits, fn, inc in prog[name]:
                for sem, val in waits:
                    e.wait_ge(sem, val)
                if fn is not None:
                    ins = fn(e)
                    ins.then_inc(inc[0], inc[1])

        with nc.Block() as block:
            @block.tensor
            def _(e):
                run(e, "pe")

            @block.scalar
            def _(e):
                run(e, "act")

            @block.vector
            def _(e):
                run(e, "dve")

            @block.gpsimd
            def _(e):
                run(e, "pool")

            @block.sync
            def _(e):
                run(e, "sp")


_DTSIZE = {F32: 4, BF16: 2}


class Arena:
    BASE = 16512
    LIMIT = 16512 + 204 * 1024

    def __init__(self, nc):
        self.nc = nc
        self.lo = self.BASE
        self.hi = self.LIMIT
        self.n = 0

    def alloc(self, name, shape, dt, side="L"):
        nb = _DTSIZE[dt]
        for d in shape[1:]:
            nb *= d
        nb = (nb + 63) // 64 * 64
        if side == "L":
            off = self.lo
            self.lo += nb
        else:
            self.hi -= nb
            off = self.hi
        assert self.lo <= self.hi, "SBUF arena overflow at %s: lo=%d hi=%d" % (name, self.lo, self.hi)
        self.n += 1
        return self.nc.alloc_sbuf_tensor_at("%s_%d" % (name, self.n), list(shape), dt, offset=off)


class K:
    def __init__(self, debug=False):
        self.debug = debug
        self.stop = 99
        self.evac_dve_only = False
        self.nc = bass.Bass("TRN2", target_bir_lowering=False)

    def act(self, out, in_, func, reads=(), writes=(), pwrites=(), bias=None, scale=None, accum=None):
        kw = {}
        if bias is not None:
            kw["bias"] = bias
        if scale is not None:
            kw["scale"] = scale
        if accum is not None:
            kw["accum_out"] = accum
        self.S.op("act", lambda e: e.activation(out=out, in_=in_, func=func, **kw), reads, writes, pwrites)

    def ts(self, eng, out, in0, s1, s2, op0, op1=None, reads=(), writes=(), pwrites=()):
        if op1 is None:
            self.S.op(eng, lambda e: e.tensor_scalar(out=out, in0=in0, scalar1=s1, scalar2=None, op0=op0),
                      reads, writes, pwrites)
        else:
            self.S.op(eng, lambda e: e.tensor_scalar(out=out, in0=in0, scalar1=s1, scalar2=s2, op0=op0, op1=op1),
                      reads, writes, pwrites)

    def tt(self, eng, out, in0, in1, op, reads=(), writes=(), pwrites=()):
        self.S.op(eng, lambda e: e.tensor_tensor(out=out, in0=in0, in1=in1, op=op), reads, writes, pwrites)

    def stt(self, out, in0, scalar, in1, op0, op1, reads=(), writes=(), pwrites=()):
        self.S.op("dve", lambda e: e.scalar_tensor_tensor(out=out, in0=in0, scalar=scalar, in1=in1, op0=op0, op1=op1),
                  reads, writes, pwrites)

    def copy(self, eng, out, in_, reads=(), writes=(), pwrites=()):
        self.S.op(eng, lambda e: e.tensor_copy(out=out, in_=in_), reads, writes, pwrites)

    def mm(self, out, pairs, reads=(), writes=(), pwrites=(), start=True, stop=True):
        def fn(e):
            n = len(pairs)
            ins = None
            for i, (l, r) in enumerate(pairs):
                ins = e.matmul(out, l, r, start=(start and i == 0), stop=(stop and i == n - 1))
            return ins
        self.S.op("pe", fn, reads, writes, pwrites)

    def load(self, eng, out, in_, buf, reads=(), partial=False):
        self.S.dma(eng, lambda e: e.dma_start(out=out, in_=in_), buf, True, reads=reads, partial=partial)

    def store(self, eng, out, in_, buf):
        self.S.dma(eng, lambda e: e.dma_start(out=out, in_=in_), buf, False)

    def build(self):
        nc = self.nc
        dram = lambda name, shape, kind="ExternalInput": nc.dram_tensor(name, shape, F32, kind=kind).ap()
        self.x_own = dram("x_own", [T, D])
        self.x_halo = dram("x_halo", [T, D])
        self.cmask_d = dram("cmask", [128, 2])
        self.w_qkv = dram("w_qkv", [12, 128, 3, 1024])
        self.w_glu = dram("w_glu", [8, 128, 2, 1024])
        self.w_gate = dram("w_gate", [8, 128, 2, 1024])
        self.w_ao = dram("w_ao", [128, 4, 1024])
        self.w_co = dram("w_co", [128, 8, 1024])
        self.w_mo = dram("w_mo", [128, 8, 1024])
        self.w_f1 = dram("w_f1", [NJ, 128, 2, 1024])
        self.w_f2 = dram("w_f2", [128, NJ, 1024])
        self.vecT_d = dram("vecT", [128, 64])
        self.wdwT_d = dram("wdwT", [128, 8 * CW])
        self.gbc_d = dram("gbc", [128, 2, 1024])
        self.bias_d = dram("bias", [12, 128, 1024])
        self.y = dram("y", [T, D], kind="ExternalOutput")
        if self.debug:
            self.dbg = {
                "d_hT": nc.dram_tensor("d_hT", [128, KC * (128 + T)], BF16, kind="ExternalOutput").ap(),
                "d_hH": nc.dram_tensor("d_hH", [128, KC * T], BF16, kind="ExternalOutput").ap(),
                "d_oT": nc.dram_tensor("d_oT", [128, 4 * T], BF16, kind="ExternalOutput").ap(),
                "d_sT": nc.dram_tensor("d_sT", [128, KC * T], BF16, kind="ExternalOutput").ap(),
                "d_mT": nc.dram_tensor("d_mT", [128, KC * T], BF16, kind="ExternalOutput").ap(),
                "d_x1": nc.dram_tensor("d_x1", [128, 16 * D], F32, kind="ExternalOutput").ap(),
            }

        with ExitStack() as st0:
            self.S = Sched(nc, st0)
            S = self.S
            A = Arena(nc)
            self.A = A
            pst = [st0.enter_context(nc.psum_tensor("ps%d" % i, [128, 1024], F32)) for i in range(4)]
            self.pst = pst
            self.bank = [pst[k // 2][:, (k % 2) * 512:(k % 2 + 1) * 512] for k in range(8)]
            self.bb = [Buf("bank%d" % k, psum=True) for k in range(8)]

            self.ident = A.alloc("ident", [128, 128], F32)
            self.onesm = A.alloc("onesm", [128, 128], BF16)
            self.onesr = A.alloc("onesr", [128, 64], BF16)
            self.nhalf = A.alloc("nhalf", [128, 512], F32)
            self.vecT = A.alloc("vecT", [128, 8, 8], F32)
            self.wdwT = A.alloc("wdwT", [128, 8, CW], F32)
            self.cmask = A.alloc("cmask", [128, 2], F32)
            self.stat = A.alloc("stat", [128, 4, 64], F32)
            self.b_const = Buf("const")
            self.b_vec, self.b_wdw, self.b_cm = Buf(), Buf(), Buf()
            S.op("pool", lambda e: e.memset(self.ident[:], 0.0), pwrites=[self.b_const])
            S.op("pool", lambda e: e.affine_select(out=self.ident[:], in_=self.ident[:], pattern=[[-1, 128]],
                                                   compare_op=ALU.not_equal, fill=1.0, base=0,
                                                   channel_multiplier=1),
                 reads=[self.b_const], pwrites=[self.b_const])
            S.op("dve", lambda e: e.memset(self.onesm[:], 1.0 / 1024.0), pwrites=[self.b_const])
            S.op("dve", lambda e: e.memset(self.onesr[:], 1.0), pwrites=[self.b_const])
            S.op("dve", lambda e: e.memset(self.nhalf[:], -0.5), pwrites=[self.b_const])
            self.load("sp", self.vecT[:, :, :].rearrange("p a b -> p (a b)"), self.vecT_d, self.b_vec)
            self.load("sp", self.wdwT[:, :, :].rearrange("p a b -> p (a b)"), self.wdwT_d, self.b_wdw)
            self.load("sp", self.cmask[:], self.cmask_d, self.b_cm)
            base_lo = A.lo
            if self.stop == 0:
                S.emit()
                return nc

            self.hT_own = A.alloc("hT_own", [128, KC, 128 + T], BF16)
            self.b_hown = Buf("hT_own")
            self.oT = A.alloc("oT", [128, 4, T], BF16)
            self.b_oT = Buf("oT")
            lo_after_hown = A.lo
            self.hT_halo = A.alloc("hT_halo", [128, KC, T], BF16)
            self.b_hhalo = Buf("hT_halo")
            lo_after_halo = A.lo
            S.renew(self.b_hown), S.renew(self.b_hhalo)
            self.wq = [A.alloc("wq%d" % i, [128, 3, 1024], BF16) for i in range(2)]
            self.b_wq = [Buf() for _ in range(2)]
            self.abias = [A.alloc("bias%d" % i, [128, 2, 512], F32) for i in range(3)]
            self.b_abias = [Buf() for _ in range(3)]
            for u in range(2):
                self.load("pool", self.wq[u][:, :, :], self.w_qkv[u], self.b_wq[u])
                self.load("sp", self.abias[u][:, :, :].rearrange("p a b -> p (a b)"), self.bias_d[u], self.b_abias[u])
            lo_after_p2w = A.lo
            self.phase1()
            if self.debug:
                self.store("sp", self.dbg["d_hT"], self.hT_own[:, :, :].rearrange("p a b -> p (a b)"), self.b_hown)
                self.store("sp", self.dbg["d_hH"], self.hT_halo[:, :, :].rearrange("p a b -> p (a b)"), self.b_hhalo)
            if self.stop == 1:
                S.emit()
                return nc
            S.barrier()
            A.lo = lo_after_p2w
            S.renew(self.b_oT)
            self.phase2()
            if self.debug:
                self.store("sp", self.dbg["d_oT"], self.oT[:, :, :].rearrange("p a b -> p (a b)"), self.b_oT)
            if self.stop == 2:
                S.emit()
                return nc
            S.barrier()
            A.lo = lo_after_hown
            self.sT = A.alloc("sT", [128, KC, T], BF16)
            self.b_sT = [[Buf() for _ in range(4)] for _ in range(KC)]
            self.wao = A.alloc("wao", [128, 4, 1024], BF16)
            self.wco = A.alloc("wco", [128, 8, 1024], BF16)
            self.wgt = [A.alloc("wgt%d" % i, [128, 2, 1024], BF16) for i in range(3)]
            self.b_wao, self.b_wco = Buf(), Buf()
            self.b_wgt = [Buf() for _ in range(3)]
            lo_after_sT = A.lo
            self.phase3()
            if self.debug:
                for c in range(KC):
                    for tt in range(4):
                        self.store("sp", self.dbg["d_sT"][:, c * T + tt * 512:c * T + (tt + 1) * 512],
                                   self.sT[:, c, tt * 512:(tt + 1) * 512], self.b_sT[c][tt])
            if self.stop == 3:
                S.emit()
                return nc
            S.barrier()
            A.lo = lo_after_sT
            self.mT = A.alloc("mT", [128, KC, T], BF16, side="R")
            self.b_mT = Buf("mT")
            S.renew(self.b_mT)
            self.wmo = A.alloc("wmo", [128, 8, 1024], BF16, side="R")
            self.b_wmo = Buf()
            self.load("pool", self.wmo[:, :, :], self.w_mo, self.b_wmo)
            self.phase4()
            if self.debug:
                self.store("sp", self.dbg["d_mT"], self.mT[:, :, :].rearrange("p a b -> p (a b)"), self.b_mT)
            if self.stop == 4:
                S.emit()
                return nc
            S.barrier()
            A.lo = base_lo
            self.x1 = A.alloc("x1", [128, 16, D], F32)
            self.b_x1 = [Buf() for _ in range(16)]
            self.wf1 = [A.alloc("wf1_%d" % i, [128, 2, 1024], BF16) for i in range(4)]
            self.b_wf1 = [Buf() for _ in range(4)]
            self.gbc6 = A.alloc("gbc6", [128, 1024], F32)
            self.b_gbc6 = Buf()
            self.h2T = [A.alloc("h2T%d" % i, [128, KC, 512], BF16) for i in range(2)]
            self.b_h2T = [Buf() for _ in range(2)]
            self.scr6 = self.alloc_norm_scratch("p6", nxs=4)
            lo_after_x1 = A.lo
            self.phase5()
            if self.debug:
                for t in range(16):
                    self.store("sp", self.dbg["d_x1"][:, t * D:(t + 1) * D], self.x1[:, t, :], self.b_x1[t])
            if self.stop == 5:
                S.emit()
                return nc
            S.barrier()
            A.lo = lo_after_x1
            A.hi = A.LIMIT
            self.phase6()
            S.emit()
        return nc

    def norm_stages(self, scr, src_fn, dst_fn, gcol, col0, banks=(0, 1, 2, 3)):
        S = self.S
        junk, xs, b_xs = scr
        nxs = len(xs)
        bt = {}
        ss, ms, rs = self.stat[:, 0, :], self.stat[:, 1, :], self.stat[:, 2, :]

        def stats(t):
            ap, buf = src_fn(t)
            cl = slice(col0 + t, col0 + t + 1)
            bt[t] = (Buf(), Buf(), Buf())
            b_ss, b_ms, b_rs = bt[t]
            self.sq_accum(junk, ap, [buf], b_ss, ss[:, cl])
            self.ts("dve", ms[:, cl], ss[:, cl], 1.0 / D, RMS_EPS, ALU.mult, ALU.add, reads=[b_ss], pwrites=[b_ms])
            self.rsqrt_act(rs[:, cl], ms[:, cl], self.stat[:, 3, cl], b_ms, b_rs)

        def scale(t):
            ap, buf = src_fn(t)
            cl = slice(col0 + t, col0 + t + 1)
            i = t % nxs
            self.act(xs[i][:, :], ap, AF.Identity, reads=[buf, bt[t][2]], writes=[b_xs[i]], scale=rs[:, cl])

        def transpose(t):
            i = t % nxs
            pb = (banks[0], banks[1]) if t % 2 == 0 else (banks[2], banks[3])
            for half in range(2):
                def tr(e, half=half, i=i, pb=pb):
                    ins = None
                    for q in range(4):
                        kc = half * 4 + q
                        ins = e.transpose(self.bank[pb[half]][:, q * 128:(q + 1) * 128],
                                          xs[i][:, kc * 128:(kc + 1) * 128], self.ident[:, :])
                    return ins
                S.op("pe", tr, reads=[b_xs[i], self.b_const], writes=[self.bb[pb[half]]])
            for kc in range(KC):
                srcp = self.bank[pb[kc // 4]][:, (kc % 4) * 128:(kc % 4 + 1) * 128]
                for (dst, dbuf) in dst_fn(t, kc):
                    if kc < 4 or self.evac_dve_only:
                        self.ts("dve", dst, srcp, self.vecT[:, kc, gcol:gcol + 1], None, ALU.mult,
                                reads=[self.bb[pb[kc // 4]], self.b_vec], pwrites=[dbuf])
                    else:
                        self.act(dst, srcp, AF.Identity, reads=[self.bb[pb[kc // 4]], self.b_vec], pwrites=[dbuf],
                                 scale=self.vecT[:, kc, gcol:gcol + 1])

        return stats, scale, transpose

    def norm_transpose(self, scr, src_fn, n_tiles, dst_fn, gcol, col0):
        stats, scale, transpose = self.norm_stages(scr, src_fn, dst_fn, gcol, col0)
        stats(0)
        if n_tiles > 1:
            stats(1)
        scale(0)
        for t in range(n_tiles):
            if t + 2 < n_tiles:
                stats(t + 2)
            if t + 1 < n_tiles:
                scale(t + 1)
            transpose(t)

    def rsqrt_act(self, rs_ap, ms_ap, ln_ap, b_ms, b_rs):
        b_ln = Buf()
        self.act(ln_ap, ms_ap, AF.Ln, reads=[b_ms], writes=[b_ln])
        self.act(rs_ap, ln_ap, AF.Exp, reads=[b_ln], pwrites=[b_rs], scale=-0.5)

    def sq_accum(self, junk, ap, reads, b_ss, ss_ap):
        tiles, bufs, k = junk
        i = k[0] % len(tiles)
        k[0] += 1
        self.act(tiles[i][:, :], ap, AF.Square, reads=reads, writes=[bufs[i]], pwrites=[b_ss], accum=ss_ap)

    def alloc_norm_scratch(self, tag, nxs=2):
        A = self.A
        junk = [A.alloc("junk%s%d" % (tag, i), [128, D], BF16) for i in range(2)]
        xs = [A.alloc("xs%s%d" % (tag, i), [128, D], F32) for i in range(nxs)]
        return (junk, [Buf() for _ in range(2)], [0]), xs, [Buf() for _ in range(nxs)]

    def phase1(self):
        A = self.A
        NS = 4
        xin = [A.alloc("xin%d" % i, [128, D], F32) for i in range(NS)]
        b_xin = [Buf() for _ in range(NS)]
        scr = self.alloc_norm_scratch("p1")
        loaded = {}

        def src(t):
            i = t % NS
            if t not in loaded:
                loaded[t] = True
                d = self.x_halo if t < 16 else self.x_own
                r = (t % 16) * 128
                self.load("sp", xin[i][:, :], d[r:r + 128, :], b_xin[i])
            return xin[i][:, :], b_xin[i]

        def dst(t, kc):
            if t < 16:
                out = [(self.hT_halo[:, kc, t * 128:(t + 1) * 128], self.b_hhalo)]
                if t == 15:
                    out.append((self.hT_own[:, kc, 0:128], self.b_hown))
                return out
            return [(self.hT_own[:, kc, 128 + (t - 16) * 128:128 + (t - 15) * 128], self.b_hown)]

        self.evac_dve_only = True
        self.norm_transpose(scr, src, 32, dst, 0, 0)
        self.evac_dve_only = False

    def phase2(self):
        S = self.S
        A = self.A
        hT_own, hT_halo = self.hT_own, self.hT_halo
        NW = 2
        NB = 3
        wq, b_wq, bias, b_bias = self.wq, self.b_wq, self.abias, self.b_abias
        qT = [A.alloc("qT%d" % i, [128, T], BF16) for i in range(2)]
        kT = [A.alloc("kT%d" % i, [128, 2 * T], BF16) for i in range(2)]
        Vt = [A.alloc("V%d" % i, [128, 32, 2, 128], BF16) for i in range(2)]
        b_q = [Buf() for _ in range(2)]
        b_k = [Buf() for _ in range(2)]
        b_v = [Buf() for _ in range(2)]
        NT = 3
        tmp = [A.alloc("tmp%d" % i, [128, 512], F32) for i in range(NT)]
        b_tmp = [Buf() for _ in range(NT)]
        NP = 6
        Pb = [A.alloc("P%d" % i, [128, 512], BF16) for i in range(NP)]
        b_P = [Buf() for _ in range(NP)]
        acc = A.alloc("acc", [128, 2, T], F32)
        b_acc = Buf()
        lnd = [A.alloc("lnd%d" % i, [64, 512], F32) for i in range(2)]
        rb = [A.alloc("rb%d" % i, [64, 512], F32) for i in range(2)]
        b_lnd = [Buf() for _ in range(2)]
        b_rb = [Buf() for _ in range(2)]
        oT, b_oT = self.oT, self.b_oT
        b_ones = [Buf(), Buf()]
        for i in range(2):
            S.op("pool", lambda e, i=i: e.memset(Vt[i][:, :, :, 64:128], 1.0), writes=[b_ones[i]])

        proj_rr = [0]
        cnt_tmp = [0]
        cnt_P = [0]
        cnt_ev = [0]

        def proj_bank():
            k = proj_rr[0] % 2
            proj_rr[0] += 1
            return k

        def params(u):
            hp, g = divmod(u, 3)
            dl = DILS[g]
            M = T // dl
            nb = M // 128
            return hp, g, u % 2, dl, M, nb, 128 + M

        def loads(u):
            if u >= 12:
                return
            self.load("pool", wq[u % NW][:, :, :], self.w_qkv[u], b_wq[u % NW])
            self.load("sp", bias[u % NB][:, :, :].rearrange("p a b -> p (a b)"), self.bias_d[u], b_bias[u % NB])

        def proj_steps(u):
            hp, g, s, dl, M, nb, KW = params(u)
            w = wq[u % NW]
            bw = b_wq[u % NW]
            steps = []
            qv = qT[s][:, :].rearrange("p (c m) -> p c m", c=dl)
            kv = kT[s][:, 0:dl * KW].rearrange("p (c m) -> p c m", c=dl)

            def evac(dst, srcp, pk, dbuf):
                if cnt_ev[0] % 3 == 2:
                    self.copy("dve", dst, srcp, reads=[self.bb[pk]], pwrites=[dbuf])
                else:
                    self.act(dst, srcp, AF.Identity, reads=[self.bb[pk]], pwrites=[dbuf])
                cnt_ev[0] += 1

            def renew_all():
                S.renew(b_q[s]), S.renew(b_k[s]), S.renew(b_v[s])
            steps.append(renew_all)
            for which, dstv, dbuf, moff in ((0, qv, b_q[s], 0), (1, kv, b_k[s], 128)):
                for tt in range(4):
                    def st(which=which, dstv=dstv, dbuf=dbuf, moff=moff, tt=tt):
                        pk = proj_bank()
                        self.mm(self.bank[pk][:, :],
                                [(w[:, which, kc * 128:(kc + 1) * 128],
                                  hT_own[:, kc, 128 + tt * 512:128 + (tt + 1) * 512]) for kc in range(KC)],
                                reads=[bw, self.b_hown], writes=[self.bb[pk]])
                        m0 = tt * 512 // dl
                        srcp = self.bank[pk][:, :].rearrange("p (m c) -> p c m", c=dl)
                        evac(dstv[:, :, moff + m0:moff + m0 + 512 // dl], srcp, pk, dbuf)
                    steps.append(st)
            nh = 128 * dl
            h0 = T - nh
            for t0 in range(0, nh, 512):
                def st(t0=t0):
                    n = min(512, nh - t0)
                    pk = proj_bank()
                    self.mm(self.bank[pk][:, 0:n],
                            [(w[:, 1, kc * 128:(kc + 1) * 128], hT_halo[:, kc, h0 + t0:h0 + t0 + n])
                             for kc in range(KC)],
                            reads=[bw, self.b_hhalo], writes=[self.bb[pk]])
                    m0 = t0 // dl
                    srcp = self.bank[pk][:, 0:n].rearrange("p (m c) -> p c m", c=dl)
                    evac(kv[:, :, m0:m0 + n // dl], srcp, pk, b_k[s])
                steps.append(st)
            nblk = dl * (nb + 1)
            for b0 in range(0, nblk, 4):
                def st(b0=b0):
                    pk = proj_bank()
                    nq4 = min(4, nblk - b0)

                    def vfn(e):
                        ins = None
                        for q in range(nq4):
                            c, j = divmod(b0 + q, nb + 1)
                            for kc in range(KC):
                                if j == 0:
                                    a0 = h0 + c
                                    lhs = hT_halo[:, kc, a0:a0 + 127 * dl + 1:dl]
                                else:
                                    a0 = 128 + dl * 128 * (j - 1) + c
                                    lhs = hT_own[:, kc, a0:a0 + 127 * dl + 1:dl]
                                ins = e.matmul(self.bank[pk][:, q * 128:(q + 1) * 128], lhs,
                                               w[:, 2, kc * 128:(kc + 1) * 128],
                                               start=(kc == 0), stop=(kc == KC - 1))
                        return ins
                    S.op("pe", vfn, reads=[bw, self.b_hown, self.b_hhalo], writes=[self.bb[pk]])
                    srcp = self.bank[pk][:, 0:nq4 * 128].rearrange("p (b d) -> p b d", d=64)
                    dst = Vt[s][:, b0:b0 + nq4, :, 0:64].rearrange("p b e d -> p (b e) d")
                    evac(dst, srcp, pk, b_v[s])
                steps.append(st)
            return steps

        def attention(u, filler, deferred=None):
            hp, g, s, dl, M, nb, KW = params(u)
            bs = bias[u % NB]
            b_bs = b_bias[u % NB]
            items = []
            for c in range(dl):
                for j0 in range(0, nb + 1, 2):
                    items.append((c, [j for j in (j0, j0 + 1) if j <= nb]))

            def cols_of(j, jj):
                lo_ = jj * 256 + (128 if j == 0 else 0)
                hi_ = jj * 256 + (128 if j == nb else 256)
                return lo_, hi_

            def scores(it, sset):
                c, jl = it
                for e_ in range(2):
                    pk = 2 + sset * 2 + e_

                    def sfn(e, e_=e_, pk=pk):
                        ins = None
                        for jj, j in enumerate(jl):
                            lo_, hi_ = cols_of(j, jj)
                            qb0 = j - 1 if j > 0 else 0
                            nq = (hi_ - lo_) // 128
                            ins = e.matmul(self.bank[pk][:, lo_:hi_],
                                           kT[s][64 * e_:64 * e_ + 64, c * KW + j * 128:c * KW + (j + 1) * 128],
                                           qT[s][64 * e_:64 * e_ + 64, c * M + qb0 * 128:c * M + (qb0 + nq) * 128],
                                           start=True, stop=True)
                        return ins
                    S.op("pe", sfn, reads=[b_q[s], b_k[s]], writes=[self.bb[pk]])

            def softmax(it, sset):
                c, jl = it
                lo_ = cols_of(jl[0], 0)[0]
                hi_ = cols_of(jl[-1], len(jl) - 1)[1]
                res = []
                for e_ in range(2):
                    pk = 2 + sset * 2 + e_
                    ti = cnt_tmp[0] % NT
                    cnt_tmp[0] += 1
                    pi = cnt_P[0] % NP
                    cnt_P[0] += 1
                    self.stt(tmp[ti][:, lo_:hi_], self.bank[pk][:, lo_:hi_], 0.125, bs[:, e_, lo_:hi_],
                             ALU.mult, ALU.add, reads=[self.bb[pk], b_bs], writes=[b_tmp[ti]])
                    if jl[0] == 0:
                        self.act(Pb[pi][:, lo_:lo_ + 128], tmp[ti][:, lo_:lo_ + 128], AF.Exp,
                                 reads=[b_tmp[ti], self.b_cm], writes=[b_P[pi]], bias=self.cmask[:, 1:2])
                        if hi_ > lo_ + 128:
                            self.act(Pb[pi][:, lo_ + 128:hi_], tmp[ti][:, lo_ + 128:hi_], AF.Exp,
                                     reads=[b_tmp[ti]], pwrites=[b_P[pi]])
                    else:
                        self.act(Pb[pi][:, lo_:hi_], tmp[ti][:, lo_:hi_], AF.Exp,
                                 reads=[b_tmp[ti]], writes=[b_P[pi]])
                    res.append(pi)
                return res

            def evac_pv(first_lin, nblocks):
                for e_ in range(2):
                    pk = 6 + e_
                    if nb >= 4:
                        c, b = divmod(first_lin, nb)
                        a0 = dl * 128 * b + c
                        n = 128 * nblocks
                        dst = acc[:, e_, a0:a0 + (n - 1) * dl + 1:dl]
                        srcp = self.bank[pk][:, 0:n]
                    else:
                        c0 = first_lin
                        dst = acc[:, e_, :].rearrange("p (a c) -> p c a", c=dl)[:, c0:c0 + nblocks, :]
                        srcp = self.bank[pk][:, 0:128 * nblocks].rearrange("p (c a) -> p c a", a=128)
                    if g == 0:
                        self.copy("dve", dst, srcp, reads=[self.bb[pk]], pwrites=[b_acc])
                    else:
                        self.tt("dve", dst, dst, srcp, ALU.add, reads=[self.bb[pk], b_acc], pwrites=[b_acc])

            def pv(it, pis):
                acts = []
                c, jl = it
                for jj, j in enumerate(jl):
                    blk = c * (nb + 1) + j
                    roles = []
                    if j > 0:
                        roles.append(("cur", j - 1, jj * 256))
                    if j < nb:
                        roles.append(("prev", j, jj * 256 + 128))
                    for role, b, pcol in roles:
                        lin = c * nb + b
                        slot = lin % 4
                        for e_ in range(2):
                            pk = 6 + e_
                            first = (role == "prev")
                            pi = pis[e_]

                            def pfn(e, pk=pk, slot=slot, blk=blk, e_=e_, pi=pi, pcol=pcol, first=first):
                                return e.matmul(self.bank[pk][:, slot * 128:(slot + 1) * 128],
                                                Vt[s][:, blk, e_, :], Pb[pi][:, pcol:pcol + 128],
                                                start=first, stop=(not first))
                            if first and slot == 0:
                                acts.append((False, lambda pfn=pfn, pi=pi, pk=pk: S.op(
                                    "pe", pfn, reads=[b_v[s], b_ones[s], b_P[pi]], writes=[self.bb[pk]])))
                            else:
                                acts.append((False, lambda pfn=pfn, pi=pi, pk=pk: S.op(
                                    "pe", pfn, reads=[b_v[s], b_ones[s], b_P[pi]], pwrites=[self.bb[pk]])))
                        if role == "cur" and (slot == 3 or lin == dl * nb - 1):
                            acts.append((True, lambda lin=lin, slot=slot: evac_pv(lin - slot, slot + 1)))
                k = 0
                while k < len(acts):
                    is_evac, fn = acts[k]
                    fn()
                    k += 1
                    if is_evac:
                        break
                rest = acts[k:]
                if not rest:
                    return None

                def run_rest():
                    for _, fn in rest:
                        fn()
                return run_rest

            nfill = len(filler)
            nit = len(items)
            fi = 0
            prev = None
            pend = None
            for idx, it in enumerate(items):
                sset = idx % 2
                scores(it, sset)
                pis = softmax(it, sset)
                tgt = (idx + 1) * nfill // nit
                while fi < tgt:
                    filler[fi]()
                    fi += 1
                if pend is not None:
                    pend()
                    pend = None
                if prev is not None:
                    pend = pv(*prev)
                prev = (it, pis)
                if idx == 0:
                    if deferred is not None:
                        deferred()
                    if g == 0:
                        S.renew(b_acc)
            if pend is not None:
                pend()
            pend = pv(*prev)
            if pend is not None:
                pend()
            while fi < nfill:
                filler[fi]()
                fi += 1

        def normalize(hp):
            k = 0
            for e_ in range(2):
                for tt in range(4):
                    cs = slice(tt * 512, (tt + 1) * 512)
                    i = k % 2
                    k += 1
                    self.act(lnd[i][:, :], acc[64:128, e_, cs], AF.Ln, reads=[b_acc], writes=[b_lnd[i]])
                    self.act(rb[i][:, :], lnd[i][:, :], AF.Exp, reads=[b_lnd[i]], writes=[b_rb[i]], scale=-1.0)
                    self.tt("dve", oT[64 * e_:64 * e_ + 64, hp, cs], acc[0:64, e_, cs], rb[i][:, :], ALU.mult,
                            reads=[b_acc, b_rb[i]], pwrites=[b_oT])

        for st in proj_steps(0):
            st()
        pending = None
        for u in range(12):
            loads(u + 2)
            filler = proj_steps(u + 1) if u + 1 < 12 else []
            attention(u, filler, pending)
            pending = None
            if u % 3 == 2:
                if u == 11:
                    normalize(u // 3)
                else:
                    pending = (lambda hp=u // 3: normalize(hp))

    def phase3(self):
        S = self.S
        A = self.A
        hT_own = self.hT_own
        sT, b_sT = self.sT, self.b_sT
        uT = A.alloc("uT", [128, KC, HW + T], BF16)
        b_uT = [Buf() for _ in range(KC)]
        lo_wg = A.lo
        wg = [A.alloc("wglu%d" % i, [128, 2, 1024], BF16) for i in range(2)]
        b_wg = [Buf() for _ in range(2)]
        sg = [A.alloc("sg%d" % i, [128, 512], F32) for i in range(2)]
        b_sg = [Buf() for _ in range(2)]
        diag = [A.alloc("diag%d" % i, [128, CW, 128], BF16) for i in range(2)]
        b_dg = [Buf() for _ in range(2)]
        sq = [A.alloc("sq%d" % i, [128, 512], BF16) for i in range(2)]
        b_sq = [Buf() for _ in range(2)]
        m2 = A.alloc("m2", [128, 512], F32)
        veps = A.alloc("veps", [128, 512], F32)
        rstd = A.alloc("rstd", [128, 512], F32)
        b_m2, b_ve, b_rstd = Buf(), Buf(), Buf()
        t1 = [A.alloc("t1_%d" % i, [128, 512], F32) for i in range(2)]
        t2 = [A.alloc("t2_%d" % i, [128, 512], F32) for i in range(2)]
        b_t1 = [Buf() for _ in range(2)]
        b_t2 = [Buf() for _ in range(2)]
        k_sg = 0
        pr = 0
        for c in range(KC):
            s = c % 2
            self.load("pool", wg[s][:, :, :], self.w_glu[c], b_wg[s])
            S.renew(b_uT[c])
            for tt in range(-1, 4):
                if tt < 0:
                    cols = slice(128 - HW, 128)
                    n = HW
                    dst = uT[:, c, 0:HW]
                else:
                    cols = slice(128 + tt * 512, 128 + (tt + 1) * 512)
                    n = 512
                    dst = uT[:, c, HW + tt * 512:HW + (tt + 1) * 512]
                pa, pb_ = pr % 4 * 2, pr % 4 * 2 + 1
                pr += 1
                for which, pk in ((0, pa), (1, pb_)):
                    self.mm(self.bank[pk][:, 0:n],
                            [(wg[s][:, which, kc * 128:(kc + 1) * 128], hT_own[:, kc, cols]) for kc in range(KC)],
                            reads=[b_wg[s], self.b_hown], writes=[self.bb[pk]])
                i = k_sg % 2
                k_sg += 1
                self.act(sg[i][:, 0:n], self.bank[pb_][:, 0:n], AF.Sigmoid, reads=[self.bb[pb_], self.b_vec],
                         writes=[b_sg[i]], bias=self.vecT[:, c, 2:3])
                self.stt(dst, self.bank[pa][:, 0:n], self.vecT[:, c, 1:2], sg[i][:, 0:n], ALU.add, ALU.mult,
                         reads=[self.bb[pa], b_sg[i], self.b_vec], pwrites=[b_uT[c]])
                if tt < 0:
                    self.ts("dve", dst, dst, self.cmask[:, 0:1], None, ALU.mult,
                            reads=[b_uT[c], self.b_cm], pwrites=[b_uT[c]])
        self.load("pool", self.wgt[0][:, :, :], self.w_gate[0], self.b_wgt[0])
        self.load("pool", self.wgt[1][:, :, :], self.w_gate[1], self.b_wgt[1])
        self.load("pool", self.wao[:, :, :], self.w_ao, self.b_wao)
        self.load("pool", self.wco[:, :, :], self.w_co, self.b_wco)
        pr = 0
        for c in range(KC):
            s = c % 2
            S.renew(b_dg[s])
            for j in range(CW):
                if j % 2 == 0:
                    self.ts("dve", diag[s][:, j, :], self.ident[:, :], self.wdwT[:, c, j:j + 1], None, ALU.mult,
                            reads=[self.b_const, self.b_wdw], pwrites=[b_dg[s]])
                else:
                    self.act(diag[s][:, j, :], self.ident[:, :], AF.Identity, reads=[self.b_const, self.b_wdw],
                             pwrites=[b_dg[s]], scale=self.wdwT[:, c, j:j + 1])
            for tt in range(4):
                pk = pr % 4
                pr += 1
                base = tt * 512 + HW - (CW - 1)
                self.mm(self.bank[pk][:, :],
                        [(diag[s][:, j, :], uT[:, c, base + j:base + j + 512]) for j in range(CW)],
                        reads=[b_dg[s], b_uT[c]], writes=[self.bb[pk]])
                self.act(sT[:, c, tt * 512:(tt + 1) * 512], self.bank[pk][:, :], AF.Identity,
                         reads=[self.bb[pk], self.b_vec], writes=[b_sT[c][tt]], bias=self.vecT[:, c, 3:4])

    def phase4(self):
        S = self.S
        A = self.A
        hT_own, oT, b_oT, sT, b_sT, mT = self.hT_own, self.oT, self.b_oT, self.sT, self.b_sT, self.mT
        wao, wco, b_wao, b_wco, wgt, b_wgt = self.wao, self.wco, self.b_wao, self.b_wco, self.wgt, self.b_wgt
        sga = [A.alloc("sga%d" % i, [128, 512], F32) for i in range(2)]
        sgc = [A.alloc("sgc%d" % i, [128, 512], F32) for i in range(2)]
        b_sga = [Buf() for _ in range(2)]
        b_sgc = [Buf() for _ in range(2)]
        sq = [A.alloc("sq%d" % i, [128, 512], BF16) for i in range(2)]
        b_sq = [Buf() for _ in range(2)]
        m2b = [A.alloc("m2_0", [128, 512], F32)] * 2
        vepsb = [A.alloc("veps0", [128, 512], F32)] * 2
        lnv = [A.alloc("lnv0", [128, 512], F32)] * 2
        rstdb = [A.alloc("rstd%d" % i, [128, 512], F32) for i in range(2)]
        nmrb = [A.alloc("nmr%d" % i, [128, 512], F32) for i in range(2)]
        b_m2s = [Buf()] * 2
        b_ves = [Buf()] * 2
        b_lnv = [Buf()] * 2
        b_rstds = [Buf() for _ in range(2)]
        b_nmrs = [Buf() for _ in range(2)]
        t1 = [A.alloc("t1_%d" % i, [128, 512], F32) for i in range(2)]
        b_t1 = [Buf() for _ in range(2)]
        cnt = {"sq": 0, "t": 0}

        def ln_a(tt):
            cs = slice(tt * 512, (tt + 1) * 512)
            pm, pe2 = 0, 1
            S.renew(self.bb[pm]), S.renew(self.bb[pe2])
            for c in range(KC):
                i = cnt["sq"] % 2
                cnt["sq"] += 1
                self.act(sq[i][:, :], sT[:, c, cs], AF.Square, reads=[b_sT[c][tt]], writes=[b_sq[i]])
                self.mm(self.bank[pm][:, :], [(self.onesm[:, :], sT[:, c, cs])],
                        reads=[self.b_const, b_sT[c][tt]], pwrites=[self.bb[pm]],
                        start=(c == 0), stop=(c == KC - 1))
                self.mm(self.bank[pe2][:, :], [(self.onesm[:, :], sq[i][:, :])],
                        reads=[self.b_const, b_sq[i]], pwrites=[self.bb[pe2]],
                        start=(c == 0), stop=(c == KC - 1))

        def ln_b(tt):
            k = tt % 2
            pm, pe2 = 0, 1
            self.act(m2b[k][:, :], self.bank[pm][:, :], AF.Square, reads=[self.bb[pm]], writes=[b_m2s[k]])
            self.stt(vepsb[k][:, :], self.bank[pe2][:, :], LN_EPS, m2b[k][:, :], ALU.add, ALU.subtract,
                     reads=[self.bb[pe2], b_m2s[k]], writes=[b_ves[k]])
            self.act(lnv[k][:, :], vepsb[k][:, :], AF.Ln, reads=[b_ves[k]], writes=[b_lnv[k]])
            self.act(rstdb[k][:, :], lnv[k][:, :], AF.Exp, reads=[b_lnv[k]], writes=[b_rstds[k]], scale=-0.5)
            self.stt(nmrb[k][:, :], self.bank[pm][:, :], -1.0, rstdb[k][:, :], ALU.mult, ALU.mult,
                     reads=[self.bb[pm], b_rstds[k]], writes=[b_nmrs[k]])

        def ln_c(tt, c):
            k = tt % 2
            cs = slice(tt * 512, (tt + 1) * 512)
            i = cnt["t"] % 2
            cnt["t"] += 1
            self.tt("dve", t1[i][:, :], sT[:, c, cs], rstdb[k][:, :], ALU.mult,
                    reads=[b_sT[c][tt], b_rstds[k]], writes=[b_t1[i]])
            self.tt("dve", t1[i][:, :], t1[i][:, :], nmrb[k][:, :], ALU.add,
                    reads=[b_nmrs[k]], writes=[b_t1[i]])
            self.act(sT[:, c, cs], t1[i][:, :], AF.Silu, reads=[b_t1[i], self.b_vec], writes=[b_sT[c][tt]],
                     scale=self.vecT[:, c, 4:5], bias=self.vecT[:, c, 5:6])

        NWG = len(wgt)
        steps = [(tt, c) for tt in range(4) for c in range(KC)]

        def wload(n):
            if 2 <= n < len(steps):
                c = steps[n][1]
                self.load("pool", wgt[n % NWG][:, :, :], self.w_gate[c], b_wgt[n % NWG])

        ln_a(0)
        ln_b(0)
        ln_a(1)
        ln_b(1)
        for c in range(KC):
            ln_c(0, c)
        wload(2)
        for n, (tt, c) in enumerate(steps):
            wload(n + 2) if n >= 1 else None
            s = n % NWG
            cs = slice(tt * 512, (tt + 1) * 512)
            hs = slice(128 + tt * 512, 128 + (tt + 1) * 512)
            i = n % 2
            p0 = (n % 2) * 4
            pga, pgc, pya, pyc = p0, p0 + 1, p0 + 2, p0 + 3
            self.mm(self.bank[pga][:, :],
                    [(wgt[s][:, 0, kc * 128:(kc + 1) * 128], hT_own[:, kc, hs]) for kc in range(KC)],
                    reads=[b_wgt[s], self.b_hown], writes=[self.bb[pga]])
            self.mm(self.bank[pgc][:, :],
                    [(wgt[s][:, 1, kc * 128:(kc + 1) * 128], hT_own[:, kc, hs]) for kc in range(KC)],
                    reads=[b_wgt[s], self.b_hown], writes=[self.bb[pgc]])
            self.mm(self.bank[pya][:, :],
                    [(wao[:, kc, c * 128:(c + 1) * 128], oT[:, kc, cs]) for kc in range(4)],
                    reads=[b_wao, b_oT], writes=[self.bb[pya]])
            self.mm(self.bank[pyc][:, :],
                    [(wco[:, kc, c * 128:(c + 1) * 128], sT[:, kc, cs]) for kc in range(KC)],
                    reads=[b_wco] + [b_sT[kc][tt] for kc in range(KC)], writes=[self.bb[pyc]])
            self.act(sga[i][:, :], self.bank[pga][:, :], AF.Sigmoid, reads=[self.bb[pga]], writes=[b_sga[i]])
            self.act(sgc[i][:, :], self.bank[pgc][:, :], AF.Sigmoid, reads=[self.bb[pgc]], writes=[b_sgc[i]])
            self.tt("dve", sga[i][:, :], self.bank[pya][:, :], sga[i][:, :], ALU.mult,
                    reads=[self.bb[pya]], writes=[b_sga[i]])
            self.stt(sgc[i][:, :], self.bank[pyc][:, :], self.vecT[:, c, 6:7], sgc[i][:, :], ALU.add, ALU.mult,
                     reads=[self.bb[pyc], self.b_vec], writes=[b_sgc[i]])
            self.tt("dve", mT[:, c, cs], sga[i][:, :], sgc[i][:, :], ALU.add,
                    reads=[b_sga[i], b_sgc[i]], pwrites=[self.b_mT])
            if tt + 1 < 4 and c in (1, 5):
                for cc in range(c - 1, c + 3):
                    ln_c(tt + 1, cc)
            if c == KC - 1 and tt + 2 < 4:
                ln_a(tt + 2)
                ln_b(tt + 2)

    def phase5(self):
        S = self.S
        A = self.A
        x1, b_x1, mT = self.x1, self.b_x1, self.mT
        wmo, b_wmo = self.wmo, self.b_wmo
        for j in range(4):
            self.load("pool", self.wf1[j][:, :, :], self.w_f1[j], self.b_wf1[j])
        self.load("sp", self.gbc6[:, :], self.gbc_d[:, 1, :], self.b_gbc6)
        gbc = A.alloc("gbc5", [128, 1024], F32)
        b_gbc = Buf()
        self.load("sp", gbc[:, :], self.gbc_d[:, 0, :], b_gbc)
        xin = [A.alloc("x5in%d" % i, [128, D], F32) for i in range(2)]
        b_xin = [Buf() for _ in range(2)]
        t1 = [A.alloc("t5_%d" % i, [128, D], F32) for i in range(2)]
        b_t1 = [Buf() for _ in range(2)]
        junk = ([A.alloc("junk5_%d" % i, [128, D], BF16) for i in range(2)], [Buf() for _ in range(2)], [0])
        b_ss, b_ms, b_rs = Buf(), Buf(), Buf()
        ss, ms, rs = self.stat[:, 0, :], self.stat[:, 1, :], self.stat[:, 2, :]
        bt5 = {}

        def front(t):
            i = t % 2
            pp = t % 4
            pa, pb_ = 2 * pp, 2 * pp + 1
            ts_ = slice(t * 128, (t + 1) * 128)
            cl = slice(32 + t, 33 + t)
            self.load("sp", xin[i][:, :], self.x_own[ts_, :], b_xin[i])
            for half, pk in ((0, pa), (1, pb_)):
                self.mm(self.bank[pk][:, :],
                        [(mT[:, kc, ts_], wmo[:, kc, half * 512:(half + 1) * 512]) for kc in range(KC)],
                        reads=[self.b_mT, b_wmo], writes=[self.bb[pk]])
            yps = self.pst[pp][:, :]
            bt5[t] = (Buf(), Buf(), Buf())
            b_ss, b_ms, b_rs = bt5[t]
            self.sq_accum(junk, yps, [self.bb[pa], self.bb[pb_]], b_ss, ss[:, cl])
            self.ts("dve", ms[:, cl], ss[:, cl], 1.0 / D, RMS_EPS, ALU.mult, ALU.add, reads=[b_ss], pwrites=[b_ms])
            self.rsqrt_act(rs[:, cl], ms[:, cl], self.stat[:, 3, cl], b_ms, b_rs)

        def back(t):
            b_rs = bt5[t][2]
            i = t % 2
            pp = t % 4
            pa, pb_ = 2 * pp, 2 * pp + 1
            cl = slice(32 + t, 33 + t)
            yps = self.pst[pp][:, :]
            self.stt(t1[i][:, :], yps, rs[:, cl], gbc[:, :], ALU.mult, ALU.mult,
                     reads=[self.bb[pa], self.bb[pb_], b_rs, b_gbc], writes=[b_t1[i]])
            self.tt("dve", x1[:, t, :], t1[i][:, :], xin[i][:, :], ALU.add,
                    reads=[b_t1[i], b_xin[i]], writes=[b_x1[t]])

        front(0)
        for t in range(16):
            if t + 1 < 16:
                front(t + 1)
            back(t)
        S.renew(self.b_h2T[0])
        self.norm_transpose(self.scr6, lambda i: (x1[:, i, :], b_x1[i]), 4,
                            lambda i, kc: [(self.h2T[0][:, kc, i * 128:(i + 1) * 128], self.b_h2T[0])], 7, 0)

    def phase6(self):
        S = self.S
        A = self.A
        x1, b_x1 = self.x1, self.b_x1
        wf1, b_wf1, gbc, b_gbc, h2T, b_h2T, scr = (self.wf1, self.b_wf1, self.gbc6, self.b_gbc6, self.h2T,
                                                      self.b_h2T, self.scr6)
        wf2 = A.alloc("wf2", [128, NJ, 1024], BF16)
        b_wf2 = Buf()
        aT = A.alloc("aT", [128, NJ, 512], BF16)
        b_aT = [Buf() for _ in range(NJ)]
        sgf = [A.alloc("sgf%d" % i, [128, 512], F32) for i in range(2)]
        b_sgf = [Buf() for _ in range(2)]
        t1 = [A.alloc("t6_%d" % i, [128, D], F32) for i in range(2)]
        b_t1 = [Buf() for _ in range(2)]
        junk = scr[0]
        b_ss, b_ms, b_rs = Buf(), Buf(), Buf()
        ss, ms, rs = self.stat[:, 0, :], self.stat[:, 1, :], self.stat[:, 2, :]
        NWF = len(wf1)
        wf2_pieces = [(0, 6), (6, 12), (12, 17), (17, 22)]
        kj = 0
        for qt in range(4):
            hq = h2T[qt % 2]
            b_hq = b_h2T[qt % 2]
            for j in range(NJ):
                s = kj % NWF
                if kj >= 4:
                    self.load("pool", wf1[s][:, :, :], self.w_f1[j], b_wf1[s])
                if qt == 0 and j < 4:
                    a, b = wf2_pieces[j]
                    self.load("pool", wf2[:, a:b, :], self.w_f2[:, a:b, :], b_wf2, partial=True)
                kj += 1
                pg, pu = 4 + 2 * (j % 2), 5 + 2 * (j % 2)
                for which, pk in ((0, pg), (1, pu)):
                    self.mm(self.bank[pk][:, :],
                            [(wf1[s][:, which, kc * 128:(kc + 1) * 128], hq[:, kc, :]) for kc in range(KC)],
                            reads=[b_wf1[s], b_hq], writes=[self.bb[pk]])
                i = j % 2
                self.act(sgf[i][:, :], self.bank[pg][:, :], AF.Silu, reads=[self.bb[pg]], writes=[b_sgf[i]])
                self.tt("dve", aT[:, j, :], sgf[i][:, :], self.bank[pu][:, :], ALU.mult,
                        reads=[b_sgf[i], self.bb[pu]], writes=[b_aT[j]])
            nxt = None
            if qt + 1 < 4:
                hn = h2T[(qt + 1) % 2]
                b_hn = b_h2T[(qt + 1) % 2]
                S.renew(b_hn)
                nxt = self.norm_stages(
                    scr, lambda i, qt=qt: (x1[:, (qt + 1) * 4 + i, :], b_x1[(qt + 1) * 4 + i]),
                    lambda i, kc, hn=hn, b_hn=b_hn: [(hn[:, kc, i * 128:(i + 1) * 128], b_hn)], 7, (qt + 1) * 4,
                    banks=(4, 5, 6, 7))
                for i4 in range(4):
                    nxt[0](i4)
                for i4 in range(4):
                    nxt[1](i4)
            for i4 in range(4):
                t = qt * 4 + i4
                i = t % 2
                pp = t % 2
                pa, pb_ = 2 * pp, 2 * pp + 1
                cl = slice(48 + t, 49 + t)
                for half, pk in ((0, pa), (1, pb_)):
                    self.mm(self.bank[pk][:, :],
                            [(aT[:, j, i4 * 128:(i4 + 1) * 128], wf2[:, j, half * 512:(half + 1) * 512])
                             for j in range(NJ)],
                            reads=b_aT + [b_wf2], writes=[self.bb[pk]])
                if nxt is not None:
                    if i4 == 0:
                        nxt[2](0)
                    if i4 + 1 < 4:
                        nxt[2](i4 + 1)
                yps = self.pst[pp][:, :]
                b_ss, b_ms, b_rs = Buf(), Buf(), Buf()
                self.sq_accum(junk, yps, [self.bb[pa], self.bb[pb_]], b_ss, ss[:, cl])
                self.ts("dve", ms[:, cl], ss[:, cl], 1.0 / D, RMS_EPS, ALU.mult, ALU.add, reads=[b_ss], pwrites=[b_ms])
                self.rsqrt_act(rs[:, cl], ms[:, cl], self.stat[:, 3, cl], b_ms, b_rs)
                self.stt(t1[i][:, :], yps, rs[:, cl], gbc[:, :], ALU.mult, ALU.mult,
                         reads=[self.bb[pa], self.bb[pb_], b_rs, b_gbc], writes=[b_t1[i]])
                self.tt("dve", x1[:, t, :], t1[i][:, :], x1[:, t, :], ALU.add,
                        reads=[b_t1[i]], writes=[b_x1[t]])
                self.store("sp", self.y[t * 128:(t + 1) * 128, :], x1[:, t, :], b_x1[t])


def _rel_bucket_np(d):
    d = np.maximum(d, 0)
    df = np.maximum(d, 1).astype(np.float32)
    large = 16 + (np.log(df / np.float32(16)) / np.float32(math.log(2048 / 16)) * np.float32(16)).astype(np.int32)
    large = np.minimum(large, 31)
    return np.where(d < 16, d, large)


def _blk(w, cols):
    n = len(cols)
    return np.ascontiguousarray(w[:, cols].reshape(KC, 128, n).transpose(1, 0, 2)).reshape(128, KC * n)


def _prep_shared(rel_bias_table, g_pre_mix, w_in, b_glu, w_dw, b_dw, g_conv_ln, b_conv_ln, w_conv_out,
                 b_conv_out, w_attn_out, w_mix_out, g_post_mix, g_pre_ffn, w_ffn_in, w_ffn_out, g_post_ffn):
    f = lambda a: np.asarray(a, dtype=np.float32)
    tab = f(rel_bias_table)
    w_in = f(w_in)[0]
    ar = np.arange(128)
    w_qkv = np.empty((12, 128, 3, 1024), np.float32)
    bias = np.empty((12, 128, 2, 4, 128), np.float32)
    kk = ar[:, None]
    qq = ar[None, :]
    for hp in range(4):
        for g in range(3):
            u = hp * 3 + g
            dl = DILS[g]
            for s in range(3):
                w_qkv[u, :, s, :] = _blk(w_in, s * 1536 + g * 512 + hp * 128 + ar)
            idx_cur = _rel_bucket_np((qq - kk) * dl)
            idx_prev = _rel_bucket_np((qq - kk + 128) * dl)
            for e in range(2):
                col = tab[:, g * 8 + hp * 2 + e]
                cur = np.where(qq >= kk, col[idx_cur], np.float32(NEG)).astype(np.float32)
                prev = np.where(kk >= qq, col[idx_prev], np.float32(NEG)).astype(np.float32)
                bias[u, :, e, 0] = cur
                bias[u, :, e, 1] = prev
                bias[u, :, e, 2] = cur
                bias[u, :, e, 3] = prev
    w_glu = np.empty((8, 128, 2, 1024), np.float32)
    w_gate = np.empty((8, 128, 2, 1024), np.float32)
    for c in range(8):
        w_glu[c, :, 0, :] = _blk(w_in, 4608 + c * 128 + ar)
        w_glu[c, :, 1, :] = _blk(w_in, 4608 + 1024 + c * 128 + ar)
        w_gate[c, :, 0, :] = _blk(w_in, 6656 + c * 128 + ar)
        w_gate[c, :, 1, :] = _blk(w_in, 7680 + c * 128 + ar)
    w_f1s = f(w_ffn_in)[0]
    w_f1 = np.empty((NJ, 128, 2, 1024), np.float32)
    for j in range(NJ):
        w_f1[j, :, 0, :] = _blk(w_f1s, j * 128 + ar)
        w_f1[j, :, 1, :] = _blk(w_f1s, FFN_H + j * 128 + ar)
    nat = lambda w, nk: np.ascontiguousarray(f(w)[0].reshape(nk, 128, -1).transpose(1, 0, 2))
    vecs = [f(g_pre_mix)[0], f(b_glu)[0][:1024], f(b_glu)[0][1024:], f(b_dw)[0], f(g_conv_ln)[0], f(b_conv_ln)[0],
            f(b_conv_out)[0], f(g_pre_ffn)[0]]
    vecT = np.ascontiguousarray(np.stack(vecs, axis=0).reshape(8, 8, 128).transpose(2, 1, 0)).reshape(128, 64)
    wdwT = np.ascontiguousarray(f(w_dw)[0].reshape(CW, 8, 128).transpose(2, 1, 0)).reshape(128, 8 * CW)
    gbc = np.ascontiguousarray(np.broadcast_to(np.stack([f(g_post_mix)[0], f(g_post_ffn)[0]])[None], (128, 2, 1024)))
    return {
        "w_qkv": w_qkv, "w_glu": w_glu, "w_gate": w_gate,
        "w_ao": nat(w_attn_out, 4), "w_co": nat(w_conv_out, 8), "w_mo": nat(w_mix_out, 8),
        "w_f1": w_f1, "w_f2": nat(w_ffn_out, NJ),
        "vecT": vecT, "wdwT": wdwT, "gbc": gbc, "bias": bias.reshape(12, 128, 1024),
    }


_NC_CACHE = {}


def _get_nc(debug):
    if debug not in _NC_CACHE:
        _NC_CACHE[debug] = K(debug=debug).build()
    return _NC_CACHE[debug]


def make_in_maps(x, shared, cores):
    x2 = np.asarray(x, dtype=np.float32).reshape(SEQ, D)
    in_maps = []
    for i in cores:
        m = dict(shared)
        m["x_own"] = np.ascontiguousarray(x2[i * T:(i + 1) * T])
        if i == 0:
            m["x_halo"] = np.zeros((T, D), np.float32)
            cm = np.array([0.0, NEG], np.float32)
        else:
            m["x_halo"] = np.ascontiguousarray(x2[(i - 1) * T:i * T])
            cm = np.array([1.0, 0.0], np.float32)
        m["cmask"] = np.ascontiguousarray(np.broadcast_to(cm[None, :], (128, 2)))
        in_maps.append(m)
    return in_maps


def kernel(x, rel_bias_table, g_pre_mix, w_in, b_glu, w_dw, b_dw, g_conv_ln, b_conv_ln, w_conv_out, b_conv_out,
           w_attn_out, w_mix_out, g_post_mix, g_pre_ffn, w_ffn_in, w_ffn_out, g_post_ffn):
    shared = _prep_shared(rel_bias_table, g_pre_mix, w_in, b_glu, w_dw, b_dw, g_conv_ln, b_conv_ln, w_conv_out,
                          b_conv_out, w_attn_out, w_mix_out, g_post_mix, g_pre_ffn, w_ffn_in, w_ffn_out, g_post_ffn)
    nc = _get_nc(False)
    in_maps = make_in_maps(x, shared, list(range(NCORES)))
    res = run_bass_kernel_spmd(nc, in_maps, core_ids=list(range(NCORES)))
    out = np.concatenate([np.asarray(r["y"], dtype=np.float32) for r in res.results], axis=0)
    return out.reshape(1, SEQ, D)
```
